# Optimizing a Trainium2 kernel written in Bass

```python
import jax, jax.numpy as jnp
from jax import lax
import numpy as np

D_MODEL = 1024
BATCH = 8
SEQ = 2048
DEPTH = 2

GRID_W = 64
CTX_LEN = 256
HEAD_DIM = 64
D_MIX = D_MODEL
LRU_WIDTH = D_MIX // 2
LRU_HEADS = LRU_WIDTH // HEAD_DIM
SC_WIDTH = D_MIX // 4
SC_GROUPS = SC_WIDTH // HEAD_DIM
CM_WIDTH = D_MIX // 4
CM_HEADS = CM_WIDTH // HEAD_DIM
CHUNK = 128
LRU_CONV = 4
LRU_CONV_LEFT = 2
SC_CONV = 3
SC_CONV_LEFT = 1
LRU_C = 8.0
D_FF = 2816
N_SUB = 3
EPS = 1e-6
D_IN = 2 * LRU_WIDTH + 3 * SC_WIDTH + 2 * CM_WIDTH
IN_OFFSETS = (LRU_WIDTH, 2 * LRU_WIDTH, 2 * LRU_WIDTH + SC_WIDTH, 2 * LRU_WIDTH + 2 * SC_WIDTH,
              2 * LRU_WIDTH + 3 * SC_WIDTH, 2 * LRU_WIDTH + 3 * SC_WIDTH + CM_WIDTH)

kernel_name = "hybrid_rglru_shortconv_chunkmlp_dit"


def rms_norm(x, g):
    xf = x.astype(jnp.float32)
    y = xf * lax.rsqrt(jnp.mean(xf * xf, axis=-1, keepdims=True) + EPS)
    return (y * g.astype(jnp.float32)).astype(x.dtype)


def sandwich(x, fn, mod3, g_pre, g_post, weight):
    shift, scale, gate = mod3
    h = rms_norm(x, g_pre) * (1 + scale) + shift
    return x + weight * gate * rms_norm(fn(h), g_post)


def swiglu(h, w_in, w_out):
    g, u = jnp.split(h @ w_in, 2, axis=-1)
    return (jax.nn.silu(g) * u) @ w_out


def dwconv(x, w, b, left):
    K = w.shape[0]
    L = x.shape[1]
    xp = jnp.pad(x, ((0, 0), (left, K - 1 - left), (0, 0)))
    y = b + w[0] * xp[:, 0:L]
    for k in range(1, K):
        y = y + w[k] * xp[:, k:k + L]
    return y


def seq_conv3(z, w, b):
    return dwconv(z, w, b, SC_CONV_LEFT)


def row_conv3(z, w, b):
    B, L, C = z.shape
    rows = L // GRID_W
    return dwconv(z.reshape(B * rows, GRID_W, C), w, b, SC_CONV_LEFT).reshape(B, L, C)


def linear_scan(a, b, h0):
    b = b.at[:, 0].add(a[:, 0] * h0)
    def combine(l, r):
        return (l[0] * r[0], r[0] * l[1] + r[1])
    return lax.associative_scan(combine, (a, b), axis=1)[1]


def lru_coeffs(xc, w_r, b_r, w_i, b_i, lam):
    B, L, _ = xc.shape
    xf = xc.astype(jnp.float32)
    xh = xf.reshape(B, L, LRU_HEADS, HEAD_DIM)
    r = jax.nn.sigmoid(jnp.einsum('blhi,hij->blhj', xh, w_r.astype(jnp.float32)).reshape(B, L, LRU_WIDTH) + b_r)
    i = jax.nn.sigmoid(jnp.einsum('blhi,hij->blhj', xh, w_i.astype(jnp.float32)).reshape(B, L, LRU_WIDTH) + b_i)
    log_a = -LRU_C * r * jax.nn.softplus(-lam.astype(jnp.float32))
    a = jnp.exp(log_a)
    bx = jnp.sqrt(-jnp.expm1(2.0 * log_a)) * (i * xf)
    return a, bx


def bidir_rglru(xc_ctx, xc_lat, w_r, b_r, w_i, b_i, lam, need_ctx_out):
    y_ctx = None
    y_lat = None
    h0 = jnp.zeros((xc_lat.shape[0], LRU_WIDTH), jnp.float32)
    for d in range(2):
        a_c, b_c = lru_coeffs(xc_ctx, w_r[d], b_r[d], w_i[d], b_i[d], lam[d])
        a_l, b_l = lru_coeffs(xc_lat, w_r[d], b_r[d], w_i[d], b_i[d], lam[d])
        if d == 1:
            a_c, b_c, a_l, b_l = (jnp.flip(t, axis=1) for t in (a_c, b_c, a_l, b_l))
        h_c = linear_scan(a_c, b_c, h0)
        h_l = linear_scan(a_l, b_l, h_c[:, -1])
        if d == 1:
            h_c, h_l = jnp.flip(h_c, axis=1), jnp.flip(h_l, axis=1)
        y_lat = h_l if y_lat is None else y_lat + h_l
        if need_ctx_out:
            y_ctx = h_c if y_ctx is None else y_ctx + h_c
    return y_ctx, y_lat


def chunk_mix(u, v, ws, bs):
    B, L, C = v.shape
    n = L // CHUNK
    vh = v.reshape(B, n, CHUNK, CM_HEADS, HEAD_DIM)
    s = jnp.einsum('hpq,bnqhd->bnphd', ws, vh) + bs.T[None, None, :, :, None]
    return u * s.reshape(B, L, C)


def head_groups(parts, y_lru, sc_w, sc_b, cm_w, cm_b, conv_fn):
    _, lru_gate, sc_bg, sc_cg, sc_x, cm_u, cm_v = parts
    o_lru = jax.nn.gelu(lru_gate) * y_lru.astype(lru_gate.dtype)
    o_sc = sc_bg * conv_fn(sc_cg * sc_x, sc_w, sc_b)
    o_cm = chunk_mix(jax.nn.gelu(cm_u), jax.nn.gelu(cm_v), cm_w, cm_b)
    return jnp.concatenate([o_lru, o_sc, o_cm], axis=-1)


def setup_inputs(seed: int = 0) -> dict:
    key = jax.random.key(seed)
    ks = jax.random.split(key, 24)
    f32 = jnp.float32

    def nrm(k, shape, scale):
        return jax.random.normal(k, shape, f32) * scale

    u = jax.random.uniform(ks[17], (DEPTH, 2, LRU_WIDTH), f32, minval=0.9, maxval=0.999)
    s = u ** (1.0 / LRU_C)
    return {
        "x": nrm(ks[0], (BATCH, SEQ, D_MODEL), 1.0),
        "c": nrm(ks[1], (BATCH, D_MODEL), 1.0),
        "ctx": nrm(ks[2], (BATCH, CTX_LEN, D_MODEL), 1.0),
        "c_ctx": nrm(ks[3], (D_MODEL,), 1.0),
        "w_mod": nrm(ks[4], (DEPTH, D_MODEL, 3 * N_SUB * D_MODEL), 0.5 * D_MODEL ** -0.5),
        "b_mod": nrm(ks[5], (DEPTH, 3 * N_SUB * D_MODEL), 0.01),
        "norm_pre": 1.0 + nrm(ks[6], (DEPTH, N_SUB, D_MODEL), 0.05),
        "norm_post": 1.0 + nrm(ks[7], (DEPTH, N_SUB, D_MODEL), 0.05),
        "ffn_w_in": nrm(ks[8], (DEPTH, 2, D_MODEL, 2 * D_FF), D_MODEL ** -0.5),
        "ffn_w_out": nrm(ks[9], (DEPTH, 2, D_FF, D_MODEL), D_FF ** -0.5),
        "w_mix_in": nrm(ks[10], (DEPTH, D_MODEL, D_IN), D_MODEL ** -0.5),
        "w_mix_out": nrm(ks[11], (DEPTH, D_MIX, D_MODEL), D_MIX ** -0.5),
        "lru_conv_w": nrm(ks[12], (DEPTH, LRU_CONV, LRU_WIDTH), LRU_CONV ** -0.5),
        "lru_conv_b": nrm(ks[13], (DEPTH, LRU_WIDTH), 0.01),
        "lru_w_r": nrm(ks[14], (DEPTH, 2, LRU_HEADS, HEAD_DIM, HEAD_DIM), HEAD_DIM ** -0.5),
        "lru_b_r": nrm(ks[15], (DEPTH, 2, LRU_WIDTH), 0.01),
        "lru_w_i": nrm(ks[16], (DEPTH, 2, LRU_HEADS, HEAD_DIM, HEAD_DIM), HEAD_DIM ** -0.5),
        "lru_b_i": nrm(ks[18], (DEPTH, 2, LRU_WIDTH), 0.01),
        "lru_lambda": jnp.log(s) - jnp.log1p(-s),
        "sc_conv_w": nrm(ks[19], (DEPTH, SC_CONV, SC_WIDTH), SC_CONV ** -0.5),
        "sc_conv_b": nrm(ks[20], (DEPTH, SC_WIDTH), 0.01),
        "cm_w_s": nrm(ks[21], (DEPTH, CM_HEADS, CHUNK, CHUNK), CHUNK ** -0.5),
        "cm_b_s": 1.0 + nrm(ks[22], (DEPTH, CM_HEADS, CHUNK), 0.05),
    }


def reference(x, c, ctx, c_ctx, w_mod, b_mod, norm_pre, norm_post, ffn_w_in, ffn_w_out,
              w_mix_in, w_mix_out, lru_conv_w, lru_conv_b, lru_w_r, lru_b_r, lru_w_i, lru_b_i,
              lru_lambda, sc_conv_w, sc_conv_b, cm_w_s, cm_b_s):
    xl, xc = x, ctx
    for l in range(DEPTH):
        last = l == DEPTH - 1
        mod_l = (jax.nn.silu(c) @ w_mod[l] + b_mod[l])[:, None, :]
        mod_c = (jax.nn.silu(c_ctx) @ w_mod[l] + b_mod[l])[None, None, :]
        ml = jnp.split(mod_l, 3 * N_SUB, axis=-1)
        mc = jnp.split(mod_c, 3 * N_SUB, axis=-1)

        ffn0 = lambda h, l=l: swiglu(h, ffn_w_in[l, 0], ffn_w_out[l, 0])
        xl = sandwich(xl, ffn0, ml[0:3], norm_pre[l, 0], norm_post[l, 0], 0.5)
        xc = sandwich(xc, ffn0, mc[0:3], norm_pre[l, 0], norm_post[l, 0], 0.5)

        hl = rms_norm(xl, norm_pre[l, 1]) * (1 + ml[4]) + ml[3]
        hc = rms_norm(xc, norm_pre[l, 1]) * (1 + mc[4]) + mc[3]
        pl = jnp.split(hl @ w_mix_in[l], IN_OFFSETS, axis=-1)
        xl_lru = dwconv(pl[0], lru_conv_w[l], lru_conv_b[l], LRU_CONV_LEFT)
        if last:
            pc = None
            xc_lru = dwconv(hc @ w_mix_in[l][:, :LRU_WIDTH], lru_conv_w[l], lru_conv_b[l], LRU_CONV_LEFT)
        else:
            pc = jnp.split(hc @ w_mix_in[l], IN_OFFSETS, axis=-1)
            xc_lru = dwconv(pc[0], lru_conv_w[l], lru_conv_b[l], LRU_CONV_LEFT)
        y_c, y_l = bidir_rglru(xc_lru, xl_lru, lru_w_r[l], lru_b_r[l], lru_w_i[l], lru_b_i[l],
                               lru_lambda[l], not last)
        mix_l = head_groups(pl, y_l, sc_conv_w[l], sc_conv_b[l], cm_w_s[l], cm_b_s[l], row_conv3) @ w_mix_out[l]
        xl = xl + ml[5] * rms_norm(mix_l, norm_post[l, 1])
        if not last:
            mix_c = head_groups(pc, y_c, sc_conv_w[l], sc_conv_b[l], cm_w_s[l], cm_b_s[l], seq_conv3) @ w_mix_out[l]
            xc = xc + mc[5] * rms_norm(mix_c, norm_post[l, 1])

        ffn1 = lambda h, l=l: swiglu(h, ffn_w_in[l, 1], ffn_w_out[l, 1])
        xl = sandwich(xl, ffn1, ml[6:9], norm_pre[l, 2], norm_post[l, 2], 0.5)
        if not last:
            xc = sandwich(xc, ffn1, mc[6:9], norm_pre[l, 2], norm_post[l, 2], 0.5)
    return xl
```

```python
import os
import contextlib
import numpy as np
import concourse.bass as bass
import concourse.mybir as mybir
from concourse.bass_utils import run_bass_kernel_spmd

F32 = mybir.dt.float32
BF16 = mybir.dt.bfloat16
AF = mybir.ActivationFunctionType
ALU = mybir.AluOpType

ENGS = ("pe", "act", "dve", "pool", "sp")


def I(_name, *args, **kwargs):
    return lambda e: getattr(e, _name)(*args, **kwargs)


class Buf:
    __slots__ = ("name", "w", "r")

    def __init__(self, name=""):
        self.name = name
        self.w = None
        self.r = []

    def handles(self):
        hs = list(self.r)
        if self.w is not None:
            hs.append(self.w)
        return hs


class Sched:
    def __init__(self, nc, n_dma_sems=32):
        self.nc = nc
        self.ops = {e: [] for e in ENGS}
        self.cnt = {e: 0 for e in ENGS}
        self.waited = {e: {} for e in ENGS}
        self.n_dma_sems = n_dma_sems
        self.dma_next = 0
        self.dma_val = [0] * n_dma_sems
        self.out_dma_handles = []

    def _add_wait(self, eng, h, waits):
        if h is None:
            return
        if h[0] == 'E' and h[1] == eng and eng == 'pe':
            return
        k = (h[0], h[1])
        if self.waited[eng].get(k, 0) >= h[2]:
            return
        self.waited[eng][k] = h[2]
        waits.append(h)

    def _deps(self, eng, reads, writes, extra):
        waits = []
        for b in reads:
            self._add_wait(eng, b.w, waits)
        for b in writes:
            self._add_wait(eng, b.w, waits)
            for r in b.r:
                self._add_wait(eng, r, waits)
        for h in extra:
            self._add_wait(eng, h, waits)
        return waits

    def _commit(self, h, reads, writes):
        for b in reads:
            b.r.append(h)
            if len(b.r) > 64:
                b.r = b.r[-64:]
        for b in writes:
            b.w = h
            b.r = []

    def op(self, eng, fns, reads=(), writes=(), extra=()):
        if not isinstance(fns, (list, tuple)):
            fns = [fns]
        waits = self._deps(eng, reads, writes, extra)
        self.cnt[eng] += 1
        h = ('E', eng, self.cnt[eng])
        self.ops[eng].append((waits, list(fns), ('E', None)))
        self._commit(h, reads, writes)
        return h

    def dma(self, eng, out_ap, in_ap, reads=(), writes=(), extra=(), is_output=False):
        waits = self._deps(eng, reads, writes, extra)
        idx = self.dma_next
        self.dma_next = (self.dma_next + 1) % self.n_dma_sems
        if self.dma_val[idx] > 0:
            self._add_wait(eng, ('D', idx, self.dma_val[idx]), waits)
        self.dma_val[idx] += 16
        h = ('D', idx, self.dma_val[idx])
        fn = I("dma_start", out=out_ap, in_=in_ap)
        self.ops[eng].append((waits, [fn], ('D', idx)))
        self._commit(h, reads, writes)
        if is_output:
            self.out_dma_handles.append(h)
        return h

    def emit(self):
        nc = self.nc
        with contextlib.ExitStack() as st:
            esem = {e: st.enter_context(nc.semaphore("s_" + e)) for e in ENGS}
            dsem = [st.enter_context(nc.semaphore("d_%d" % i)) for i in range(self.n_dma_sems)]
            block = st.enter_context(nc.Block())

            def semof(h):
                return esem[h[1]] if h[0] == 'E' else dsem[h[1]]

            def run(eng_name, eng):
                for waits, fns, inc in self.ops[eng_name]:
                    for h in waits:
                        eng.wait_ge(semof(h), h[2])
                    ins = None
                    for fn in fns:
                        ins = fn(eng)
                    if inc[0] == 'E':
                        ins.then_inc(esem[eng_name], 1)
                    else:
                        ins.then_inc(dsem[inc[1]], 16)
                if eng_name == 'sp':
                    for h in self.out_dma_handles:
                        eng.wait_ge(semof(h), h[2])

            @block.tensor
            def _(e):
                run('pe', e)

            @block.scalar
            def _(e):
                run('act', e)

            @block.vector
            def _(e):
                run('dve', e)

            @block.gpsimd
            def _(e):
                run('pool', e)

            @block.sync
            def _(e):
                run('sp', e)


P = 128
D = 1024
KD = 8
SEQ = 2048
CTX = 256
NTOK = SEQ + CTX
DFF = 2816
NF = 22
NM = 18
L = 2
EPS = 1e-6
BLOCKS = [(0, 256, 1)] + [(256 + 512 * i, 512, 0) for i in range(4)]
NBLK = len(BLOCKS)

PV_LAYOUT = [("bmod", 72), ("npre", 24), ("npost", 24), ("lcw", 16), ("lcb", 4), ("br", 8), ("bi", 8),
             ("lam", 8), ("scw", 6), ("scb", 2)]
PV_OFF = {}
_o = 0
for _n, _w in PV_LAYOUT:
    PV_OFF[_n] = _o
    _o += _w
PV_PER_LAYER = _o
NPV = PV_PER_LAYER * L

OC = 2
OL = 261
XPW = 2312


def xp_off(c0):
    return OC + c0 if c0 < CTX else OL + (c0 - CTX)


def build(nstage=6, debug_ctx=False):
    nc = bass.Bass("TRN2", target_bir_lowering=False)
    S = Sched(nc)
    st = contextlib.ExitStack()

    def din(name, shape, dt=F32):
        return nc.dram_tensor(name, list(shape), dt, kind="ExternalInput").ap()

    def dscr(name, shape, dt=BF16):
        return nc.dram_tensor(name, list(shape), dt, kind="Internal").ap()

    x_d = din("x", [SEQ, D])
    ctx_d = din("ctx", [CTX, D])
    out_d = nc.dram_tensor("out", [SEQ, D], F32, kind="ExternalOutput").ap()
    if debug_ctx:
        outc_d = nc.dram_tensor("outc", [CTX, D], F32, kind="ExternalOutput").ap()
    ident_d = din("ident", [P, P])
    cv_d = din("cv", [P, 16])
    pv_d = din("pv", [P, NPV])
    wmod_d = din("wmod", [L, D, 9 * D])
    win_d = din("win", [L, 2, NF * P, 2048])
    wout_d = din("wout", [L, 2, KD * P, DFF])
    wmi_d = din("wmi", [L, NM * P, 1024])
    wmo_d = din("wmo", [L, KD * P, 1024])
    bd_d = din("bd", [L, P, 2048])
    wst_d = din("wst", [L, P, 512])
    bsb_d = din("bsb", [L, P, 256])

    win_b = dscr("win_b", [L, 2, NF * P, 2048])
    wout_b = dscr("wout_b", [L, 2, KD * P, DFF])
    wmi_b = dscr("wmi_b", [L, NM * P, 1024])
    wmo_b = dscr("wmo_b", [L, KD * P, 1024])
    osc = dscr("osc", [KD, P, NTOK])

    def sb(name, shape, dt=F32):
        return st.enter_context(nc.sbuf_tensor(name, list(shape), dt))

    XT = sb("XT", [P, KD, NTOK])
    HT = sb("HT", [P, KD, NTOK], BF16)
    SQ = sb("SQ", [P, KD, 512], BF16)
    RSTD = sb("RSTD", [P, 2, 512])
    TMP = sb("TMP", [P, 4, 512])
    WM = sb("WM", [P, 4, 1024], BF16)
    BD = sb("BD", [P, 16, P], BF16)
    WST = sb("WST", [P, 4, P], BF16)
    BSB = sb("BSB", [P, 2, P])
    VT = sb("VT", [P, 512], BF16)
    ID = sb("ID", [P, P])
    ONES = sb("ONES", [P, P], BF16)
    PV = sb("PV", [P, NPV])
    SCV = sb("SCV", [P, KD, 2])
    MODV = sb("MODV", [P, 72, 2])
    MODP = sb("MODP", [P, L * 3 * 3 * 16])
    CA = sb("CA", [P, 16])
    UW = 14592
    U = sb("U", [P, UW])
    PS = [st.enter_context(nc.psum_tensor("ps%d" % i, [P, 512], F32)) for i in range(8)]
    bPS = [Buf("ps%d" % i) for i in range(8)]

    def uv(off_b, nbytes, dt):
        a = U[:, off_b // 4:(off_b + nbytes) // 4]
        return a.bitcast(BF16) if dt == BF16 else a

    KB = 1024
    ACTT = uv(0, 22 * KB, BF16).rearrange("p (f n) -> p f n", f=NF)
    YT = uv(22 * KB, 16 * KB, F32).rearrange("p (k n) -> p k n", k=KD)
    WIN = uv(38 * KB, 8 * KB, BF16).rearrange("p (s n) -> p s n", s=2)
    WOUT = uv(46 * KB, 11 * KB, BF16).rearrange("p (s n) -> p s n", s=4)
    WIN2 = WM[:].rearrange("p (s a) n -> p s (a n)", s=2)
    XIN = uv(0, 8 * KB, F32).rearrange("p (s n) -> p s n", s=2)
    WMS = uv(8 * KB, 32 * KB, F32).rearrange("p (s k n) -> p s k n", s=2, k=KD)
    XP = uv(0, XPW * 4, F32)
    o1 = 9248
    XC = uv(o1, 9216, F32)
    A_ = uv(o1 + 9216, 9216, F32)
    BX = uv(o1 + 2 * 9216, 9216, F32)
    YY = uv(o1 + 3 * 9216, 9216, F32)
    XCB = uv(o1 + 4 * 9216, 4608, BF16)
    OCH = uv(o1 + 4 * 9216 + 4608, 4608, BF16)
    YT2 = YT
    OBLK = uv(0, 8 * KB, BF16).rearrange("p (k n) -> p k n", k=KD)
    WMO = uv(38 * KB, 16 * KB, BF16).rearrange("p (j n) -> p j n", j=KD)

    Ubufs = []

    def new_phase(names):
        hs = []
        for b in Ubufs:
            hs.extend(b.handles())
        best = {}
        for h in hs:
            k = (h[0], h[1])
            if k not in best or best[k][2] < h[2]:
                best[k] = h
        del Ubufs[:]
        out = []
        for n in names:
            b = Buf(n)
            b.r = list(best.values())
            Ubufs.append(b)
            out.append(b)
        return out

    bXT = [Buf("xt%d" % i) for i in range(NBLK)]
    bHT = [Buf("ht%d" % i) for i in range(NBLK)]
    bSQ, bRS0, bRS1 = Buf("sq"), Buf("rs0"), Buf("rs1")
    bTMP = [Buf("tmp%d" % i) for i in range(4)]
    bWM = [Buf("wm%d" % i) for i in range(4)]
    bBD, bWST, bBSB, bVT = Buf(), Buf(), Buf(), Buf()
    bID, bONES, bPV, bSCV, bMODV, bMODP, bCA = Buf(), Buf(), Buf(), Buf(), Buf(), Buf(), Buf()
    b_win = [[Buf() for s in range(2)] for l in range(L)]
    b_wout = [[Buf() for s in range(2)] for l in range(L)]
    b_wmi = [Buf() for l in range(L)]
    b_wmo = [Buf() for l in range(L)]
    bOSC = [Buf() for c in range(KD)]

    def pvc(name, l, j, n=1):
        o = l * PV_PER_LAYER + PV_OFF[name] + j
        return PV[:, o:o + n]

    def modp(l, s, kind, k, t):
        o = (((l * 3 + s) * 3 + kind) * 8 + k) * 2 + t
        return MODP[:, o:o + 1]

    def cast_ffn(l, s):
        S.dma('pool', win_b[l, s], win_d[l, s], writes=[b_win[l][s]])
        S.dma('pool', wout_b[l, s].rearrange("r (a b) -> r a b", b=1408),
              wout_d[l, s].rearrange("r (a b) -> r a b", b=1408), writes=[b_wout[l][s]])

    def cast_mix(l):
        S.dma('pool', wmi_b[l], wmi_d[l], writes=[b_wmi[l]])
        S.dma('pool', wmo_b[l], wmo_d[l], writes=[b_wmo[l]])

    cast_ffn(0, 0)

    S.dma('sp', ID[:], ident_d, writes=[bID])
    S.dma('sp', PV[:], pv_d, writes=[bPV])
    S.dma('sp', SCV[:].rearrange("p k t -> p (k t)"), cv_d, writes=[bSCV])
    S.op('dve', I("memset", ONES[:], 1.0), writes=[bONES])
    S.op('act', I("activation", out=SCV[:], in_=SCV[:], func=AF.Silu), reads=[bSCV], writes=[bSCV])
    for l in range(L):
        S.op('act', I("activation", out=CA[:, l * 8:(l + 1) * 8], in_=pvc("lam", l, 0, 8), func=AF.Exp, scale=-1.0),
             reads=[bPV], writes=[bCA])
    S.op('act', I("activation", out=CA[:], in_=CA[:], func=AF.Ln, bias=1.0), reads=[bCA], writes=[bCA])
    S.op('dve', I("tensor_scalar", CA[:], CA[:], -8.0, None, ALU.mult), reads=[bCA], writes=[bCA])

    (bXIN0, bXIN1, bWMS0, bWMS1) = new_phase(["xin0", "xin1", "wms0", "wms1"])
    bXIN = [bXIN0, bXIN1]
    bWMS = [bWMS0, bWMS1]
    cpy = 0
    for tt in range(NTOK // P):
        src = ctx_d[tt * P:(tt + 1) * P, :] if tt < 2 else x_d[(tt - 2) * P:(tt - 1) * P, :]
        slot = tt % 2
        blk = 0 if tt < 2 else 1 + (tt - 2) // 4
        S.dma('sp', XIN[:, slot, :], src, writes=[bXIN[slot]])
        for half in range(2):
            pb = (tt * 2 + half) % 4
            S.op('pe', [I("transpose", PS[pb][:, k * P:(k + 1) * P], XIN[:, slot, (half * 4 + k) * P:(half * 4 + k + 1) * P], ID[:]) for k in range(4)],
                 reads=[bXIN[slot], bID], writes=[bPS[pb]])
            dst = XT[:, half * 4:(half + 1) * 4, tt * P:(tt + 1) * P]
            srcp = PS[pb][:].rearrange("p (k t) -> p k t", k=4)
            if cpy % 2 == 0:
                S.op('dve', I("tensor_copy", dst, srcp), reads=[bPS[pb]], writes=[bXT[blk]])
            else:
                S.op('act', I("activation", out=dst, in_=srcp, func=AF.Copy), reads=[bPS[pb]], writes=[bXT[blk]])
            cpy += 1

    MODPS = PS[4][:, 0:144].rearrange("p (m t) -> p m t", t=2)
    SLAB = 512
    for l in range(L):
        nsl = 9 * D // SLAB
        for sl in range(nsl):
            slot = sl % 2
            S.dma('sp', WMS[:, slot], wmod_d[l].rearrange("(k p) m -> p k m", p=P)[:, :, sl * SLAB:(sl + 1) * SLAB],
                  writes=[bWMS[slot]])
            fns = []
            for mm in range(SLAB // P):
                m = sl * (SLAB // P) + mm
                for k in range(KD):
                    fns.append(I("matmul", MODPS[:, m, :], WMS[:, slot, k, mm * P:(mm + 1) * P], SCV[:, k, :], start=(k == 0), stop=(k == KD - 1)))
            S.op('pe', fns, reads=[bWMS[slot], bSCV], writes=[bPS[4]])
        for t in range(2):
            S.op('dve', I("tensor_tensor", MODV[:, :, t], MODPS[:, :, t], pvc("bmod", l, 0, 72), ALU.add),
                 reads=[bPS[4], bPV], writes=[bMODV])
        for s in range(3):
            wgt = 1.0 if s == 1 else 0.5
            for t in range(2):
                o = (((l * 3 + s) * 3 + 0) * 8) * 2
                dstA = MODP[:, o:o + 16].rearrange("p (k t) -> p k t", t=2)[:, :, t]
                dstS = MODP[:, o + 16:o + 32].rearrange("p (k t) -> p k t", t=2)[:, :, t]
                dstG = MODP[:, o + 32:o + 48].rearrange("p (k t) -> p k t", t=2)[:, :, t]
                sh = MODV[:, (3 * s) * 8:(3 * s) * 8 + 8, t]
                sc = MODV[:, (3 * s + 1) * 8:(3 * s + 1) * 8 + 8, t]
                ga = MODV[:, (3 * s + 2) * 8:(3 * s + 2) * 8 + 8, t]
                S.op('dve', I("scalar_tensor_tensor", dstA, sc, 1.0, pvc("npre", l, s * 8, 8), ALU.add, ALU.mult), reads=[bMODV, bPV], writes=[bMODP])
                S.op('dve', I("tensor_copy", dstS, sh), reads=[bMODV], writes=[bMODP])
                S.op('dve', I("scalar_tensor_tensor", dstG, ga, wgt, pvc("npost", l, s * 8, 8), ALU.mult, ALU.mult), reads=[bMODV, bPV], writes=[bMODP])

    def rstd_from(ssp, bssp, n, slot):
        brs = bRS0 if slot == 0 else bRS1
        S.op('act', I("activation", out=RSTD[:, slot, :n], in_=ssp[:, :n], func=AF.Sqrt, scale=1.0 / D, bias=PVEPS),
             reads=[bssp, bEPS], writes=[brs])
        S.op('dve', I("reciprocal", RSTD[:, slot, :n], RSTD[:, slot, :n]), reads=[brs], writes=[brs])
        return brs

    EPSB = sb("EPSB", [P, 1])
    bEPS = Buf()
    S.op('dve', I("memset", EPSB[:], EPS), writes=[bEPS])
    PVEPS = EPSB[:, 0:1]

    def prenorm_sq(l, s, bi):
        c0, n, t = BLOCKS[bi]
        S.op('act', I("activation", out=SQ[:, :, :n], in_=XT[:, :, c0:c0 + n], func=AF.Square),
             reads=[bXT[bi]], writes=[bSQ])

    def prenorm_rest(l, s, bi):
        c0, n, t = BLOCKS[bi]
        S.op('pe', [I("matmul", PS[6][:, :n], ONES[:], SQ[:, k, :n], start=(k == 0), stop=(k == KD - 1)) for k in range(KD)],
             reads=[bSQ, bONES], writes=[bPS[6]])
        brs = rstd_from(PS[6], bPS[6], n, 0)
        for k in range(KD):
            ts = 2 + (k % 2)
            S.op('dve', I("scalar_tensor_tensor", TMP[:, ts, :n], XT[:, k, c0:c0 + n], modp(l, s, 0, k, t), RSTD[:, 0, :n], ALU.mult, ALU.mult),
                 reads=[bXT[bi], brs, bMODP], writes=[bTMP[ts]])
            S.op('act', I("activation", out=HT[:, k, c0:c0 + n], in_=TMP[:, ts, :n], func=AF.Identity,
                                                          bias=modp(l, s, 1, k, t), scale=1.0),
                 reads=[bTMP[ts], bMODP], writes=[bHT[bi]])

    def postnorm_update(l, s, bi, bYT, YTv):
        c0, n, t = BLOCKS[bi]
        S.op('pe', [I("matmul", PS[7][:, :n], ONES[:], SQ[:, k, :n], start=(k == 0), stop=(k == KD - 1)) for k in range(KD)],
             reads=[bSQ, bONES], writes=[bPS[7]])
        brs = rstd_from(PS[7], bPS[7], n, 1)
        for k in range(KD):
            ts = k % 2
            S.op('dve', I("scalar_tensor_tensor", TMP[:, ts, :n], YTv[:, k, :n], modp(l, s, 2, k, t), RSTD[:, 1, :n], ALU.mult, ALU.mult),
                 reads=[bYT, brs, bMODP], writes=[bTMP[ts]])
            S.op('dve', I("tensor_tensor", XT[:, k, c0:c0 + n], XT[:, k, c0:c0 + n], TMP[:, ts, :n], ALU.add),
                 reads=[bTMP[ts]], writes=[bXT[bi]])

    def ffn(l, s, skip_ctx=False):
        wi = s // 2
        (bACTT, bYT, bWIN0, bWIN1, bWOUT0, bWOUT1, bWOUT2, bWOUT3) = new_phase(["actt", "yt", "win0", "win1", "wout0", "wout1", "wout2", "wout3"])
        bWIN = [bWIN0, bWIN1]
        bWOUT = [bWOUT0, bWOUT1, bWOUT2, bWOUT3]
        blks = [bi for bi in range(NBLK) if not (skip_ctx and BLOCKS[bi][2] == 1)]
        nwin = 0
        nwout = 0
        prenorm_sq(l, s, blks[0])
        prenorm_rest(l, s, blks[0])
        for ii, bi in enumerate(blks):
            c0, n, t = BLOCKS[bi]
            nxt = blks[ii + 1] if ii + 1 < len(blks) else None
            for f in range(NF):
                slot = nwin % 4
                nwin += 1
                WINs = WIN[:, slot, :] if slot < 2 else WIN2[:, slot - 2, :]
                bWs = [bWIN[slot]] if slot < 2 else [bWM[2 * (slot - 2)], bWM[2 * (slot - 2) + 1]]
                S.dma('sp', WINs, win_b[l, wi, f * P:(f + 1) * P, :], reads=[b_win[l][wi]], writes=bWs)
                gp = f % 2
                up = 2 + f % 2
                S.op('pe', [I("matmul", PS[gp][:, :n], WINs[:, k * 256:k * 256 + P], HT[:, k, c0:c0 + n],
                                                                     start=(k == 0), stop=(k == KD - 1)) for k in range(KD)],
                     reads=bWs + [bHT[bi]], writes=[bPS[gp]])
                S.op('pe', [I("matmul", PS[up][:, :n], WINs[:, k * 256 + P:k * 256 + 2 * P], HT[:, k, c0:c0 + n],
                                                                     start=(k == 0), stop=(k == KD - 1)) for k in range(KD)],
                     reads=bWs + [bHT[bi]], writes=[bPS[up]])
                ts = f % 2
                S.op('act', I("activation", out=TMP[:, ts, :n], in_=PS[gp][:, :n], func=AF.Silu),
                     reads=[bPS[gp]], writes=[bTMP[ts]])
                S.op('dve', I("tensor_tensor", ACTT[:, f, :n], TMP[:, ts, :n], PS[up][:, :n], ALU.mult),
                     reads=[bTMP[ts], bPS[up]], writes=[bACTT])
                if f == 15 and nxt is not None:
                    prenorm_sq(l, s, nxt)
            if nxt is not None:
                prenorm_rest(l, s, nxt)
            for j in range(KD):
                yp = 4 + j % 2
                for hf in range(2):
                    slot = nwout % 4
                    nwout += 1
                    S.dma('sp', WOUT[:, slot, :], wout_b[l, wi, j * P:(j + 1) * P, hf * 1408:(hf + 1) * 1408], reads=[b_wout[l][wi]], writes=[bWOUT[slot]])
                    S.op('pe', [I("matmul", PS[yp][:, :n], WOUT[:, slot, f * P:(f + 1) * P], ACTT[:, hf * 11 + f, :n],
                                                                         start=(hf == 0 and f == 0), stop=(hf == 1 and f == 10)) for f in range(11)],
                         reads=[bWOUT[slot], bACTT], writes=[bPS[yp]])
                S.op('act', I("activation", out=YT[:, j, :n], in_=PS[yp][:, :n], func=AF.Copy),
                     reads=[bPS[yp]], writes=[bYT])
                S.op('dve', I("tensor_tensor", SQ[:, j, :n], YT[:, j, :n], YT[:, j, :n], ALU.mult),
                     reads=[bYT], writes=[bSQ])
            postnorm_update(l, s, bi, bYT, YT)

    def mixer(l):
        last = (l == L - 1)
        s = 1
        (bXP, bXC, bA, bBX, bYY, bXCB, bOCH) = new_phase(["xp", "xc", "a", "bx", "yy", "xcb", "och"])
        S.dma('pool', BD[:].rearrange("p a b -> p (a b)"), bd_d[l], writes=[bBD])
        S.dma('pool', WST[:].rearrange("p a b -> p (a b)"), wst_d[l], writes=[bWST])
        S.dma('sp', BSB[:].rearrange("p a b -> p (a b)"), bsb_d[l], writes=[bBSB])
        for bi in range(NBLK):
            prenorm_sq(l, s, bi)
            prenorm_rest(l, s, bi)
        S.op('dve', I("memset", XP[:], 0.0), writes=[bXP])
        wm_n = [0]

        def load_wm(m):
            slot = wm_n[0] % 4
            wm_n[0] += 1
            S.dma('sp', WM[:, slot, :], wmi_b[l, m * P:(m + 1) * P, :], reads=[b_wmi[l]], writes=[bWM[slot]])
            return slot

        psn = [0]

        def proj(slot, bi):
            c0, n, t = BLOCKS[bi]
            pb = psn[0] % 4
            psn[0] += 1
            S.op('pe', [I("matmul", PS[pb][:, :n], WM[:, slot, k * P:(k + 1) * P], HT[:, k, c0:c0 + n],
                                                start=(k == 0), stop=(k == KD - 1)) for k in range(KD)],
                 reads=[bWM[slot], bHT[bi]], writes=[bPS[pb]])
            return pb

        blks_all = list(range(NBLK))
        blks_out = [bi for bi in blks_all if not (last and BLOCKS[bi][2] == 1)]
        segs = [(0, CTX, OC), (CTX, SEQ, OL)]

        for c in range(4):
            sx = load_wm(c)
            sg = load_wm(4 + c)
            for bi in blks_all:
                c0, n, t = BLOCKS[bi]
                pb = proj(sx, bi)
                S.op('act', I("activation", out=XP[:, xp_off(c0):xp_off(c0) + n], in_=PS[pb][:, :n], func=AF.Copy),
                     reads=[bPS[pb]], writes=[bXP])
            for (t0, n, o) in segs:
                S.op('dve', I("tensor_scalar", XC[:, t0:t0 + n], XP[:, o - 2:o - 2 + n], pvc("lcw", l, c * 4 + 0),
                                                                      pvc("lcb", l, c), ALU.mult, ALU.add),
                     reads=[bXP, bPV], writes=[bXC])
                for tap in range(1, 4):
                    S.op('dve', I("scalar_tensor_tensor", XC[:, t0:t0 + n], XP[:, o - 2 + tap:o - 2 + tap + n], pvc("lcw", l, c * 4 + tap), XC[:, t0:t0 + n], ALU.mult, ALU.add),
                         reads=[bXP, bPV], writes=[bXC])
            S.op('act', I("activation", out=XCB[:], in_=XC[:], func=AF.Copy), reads=[bXC], writes=[bXCB])
            T3 = XP[:, 0:NTOK]
            for d in range(2):
                for bi in blks_all:
                    c0, n, t = BLOCKS[bi]
                    pr = psn[0] % 4
                    psn[0] += 1
                    S.op('pe', I("matmul", PS[pr][:, :n], BD[:, (d * 2 + 0) * 4 + c, :], XCB[:, c0:c0 + n], start=True, stop=True),
                         reads=[bBD, bXCB], writes=[bPS[pr]])
                    S.op('act', I("activation", out=A_[:, c0:c0 + n], in_=PS[pr][:, :n], func=AF.Sigmoid,
                                                                             bias=pvc("br", l, d * 4 + c), scale=1.0),
                         reads=[bPS[pr], bPV], writes=[bA])
                    pi = psn[0] % 4
                    psn[0] += 1
                    S.op('pe', I("matmul", PS[pi][:, :n], BD[:, (d * 2 + 1) * 4 + c, :], XCB[:, c0:c0 + n], start=True, stop=True),
                         reads=[bBD, bXCB], writes=[bPS[pi]])
                    S.op('act', I("activation", out=BX[:, c0:c0 + n], in_=PS[pi][:, :n], func=AF.Sigmoid,
                                                                             bias=pvc("bi", l, d * 4 + c), scale=1.0),
                         reads=[bPS[pi], bPV], writes=[bBX])
                cao = l * 8 + d * 4 + c
                S.op('act', I("activation", out=A_[:], in_=A_[:], func=AF.Exp, scale=CA[:, cao:cao + 1]),
                     reads=[bA, bCA], writes=[bA])
                S.op('dve', I("tensor_tensor", BX[:], BX[:], XC[:], ALU.mult), reads=[bBX, bXC], writes=[bBX])
                S.op('act', I("activation", out=T3, in_=A_[:], func=AF.Square), reads=[bA], writes=[bXP])
                S.op('act', I("activation", out=T3, in_=T3, func=AF.Sqrt, scale=-1.0, bias=1.0), reads=[bXP], writes=[bXP])
                S.op('dve', I("tensor_tensor", BX[:], BX[:], T3, ALU.mult), reads=[bBX, bXP], writes=[bBX])
                Hd = YY if d == 0 else XP[:, 0:NTOK]
                bH = bYY if d == 0 else bXP
                if d == 0:
                    S.op('dve', I("tensor_tensor_scan", YY[:, 0:CTX], A_[:, 0:CTX], BX[:, 0:CTX], 0.0, ALU.mult, ALU.add),
                         reads=[bA, bBX], writes=[bYY])
                    S.op('dve', I("tensor_tensor_scan", YY[:, CTX:NTOK], A_[:, CTX:NTOK], BX[:, CTX:NTOK], YY[:, CTX - 1:CTX], ALU.mult, ALU.add),
                         reads=[bA, bBX, bYY], writes=[bYY])
                else:
                    S.op('dve', I("tensor_tensor_scan", T3[:, 0:CTX][:, ::-1], A_[:, 0:CTX][:, ::-1], BX[:, 0:CTX][:, ::-1], 0.0, ALU.mult, ALU.add),
                         reads=[bA, bBX], writes=[bXP])
                    S.op('dve', I("tensor_tensor_scan", T3[:, CTX:NTOK][:, ::-1], A_[:, CTX:NTOK][:, ::-1], BX[:, CTX:NTOK][:, ::-1], T3[:, 0:1],
                                                               ALU.mult, ALU.add),
                         reads=[bA, bBX, bXP], writes=[bXP])
                    S.op('dve', I("tensor_tensor", YY[:], YY[:], T3, ALU.add), reads=[bXP], writes=[bYY])
            S.op('dve', I("memset", XP[:], 0.0), writes=[bXP])
            for bi in blks_out:
                c0, n, t = BLOCKS[bi]
                pb = proj(sg, bi)
                ts = bi % 2
                S.op('act', I("activation", out=TMP[:, ts, :n], in_=PS[pb][:, :n], func=AF.Gelu_apprx_tanh),
                     reads=[bPS[pb]], writes=[bTMP[ts]])
                S.op('dve', I("tensor_tensor", OCH[:, c0:c0 + n], TMP[:, ts, :n], YY[:, c0:c0 + n], ALU.mult),
                     reads=[bTMP[ts], bYY], writes=[bOCH])
            S.dma('sp', osc[c], OCH[:], reads=[bOCH], writes=[bOSC[c]])

        for c in range(2):
            sbg = load_wm(8 + c)
            scg = load_wm(10 + c)
            sxx = load_wm(12 + c)
            for bi in blks_out:
                c0, n, t = BLOCKS[bi]
                p1 = proj(sxx, bi)
                ts = bi % 2
                S.op('act', I("activation", out=TMP[:, ts, :n], in_=PS[p1][:, :n], func=AF.Copy),
                     reads=[bPS[p1]], writes=[bTMP[ts]])
                p2 = proj(scg, bi)
                S.op('dve', I("tensor_tensor", XP[:, xp_off(c0):xp_off(c0) + n], TMP[:, ts, :n], PS[p2][:, :n], ALU.mult),
                     reads=[bTMP[ts], bPS[p2]], writes=[bXP])
                p3 = proj(sbg, bi)
                S.op('act', I("activation", out=A_[:, c0:c0 + n], in_=PS[p3][:, :n], func=AF.Copy),
                     reads=[bPS[p3]], writes=[bA])
            w0, w1, w2, bb = pvc("scw", l, c * 3 + 0), pvc("scw", l, c * 3 + 1), pvc("scw", l, c * 3 + 2), pvc("scb", l, c)
            if not last:
                S.op('dve', I("tensor_scalar", XC[:, 0:CTX], XP[:, OC:OC + CTX], w1, bb, ALU.mult, ALU.add), reads=[bXP, bPV], writes=[bXC])
                S.op('dve', I("scalar_tensor_tensor", XC[:, 0:CTX], XP[:, OC - 1:OC - 1 + CTX], w0, XC[:, 0:CTX], ALU.mult, ALU.add),
                     reads=[bXP, bPV], writes=[bXC])
                S.op('dve', I("scalar_tensor_tensor", XC[:, 0:CTX], XP[:, OC + 1:OC + 1 + CTX], w2, XC[:, 0:CTX], ALU.mult, ALU.add),
                     reads=[bXP, bPV], writes=[bXC])
            PRL = XP[:, OL:OL + SEQ].rearrange("p (r w) -> p r w", w=64)
            CVL = XC[:, CTX:NTOK].rearrange("p (r w) -> p r w", w=64)
            S.op('dve', I("tensor_scalar", XC[:, CTX:NTOK], XP[:, OL:OL + SEQ], w1, bb, ALU.mult, ALU.add), reads=[bXP, bPV], writes=[bXC])
            S.op('dve', I("scalar_tensor_tensor", CVL[:, :, 1:64], PRL[:, :, 0:63], w0, CVL[:, :, 1:64], ALU.mult, ALU.add),
                 reads=[bXP, bPV], writes=[bXC])
            S.op('dve', I("scalar_tensor_tensor", CVL[:, :, 0:63], PRL[:, :, 1:64], w2, CVL[:, :, 0:63], ALU.mult, ALU.add),
                 reads=[bXP, bPV], writes=[bXC])
            lo = CTX if last else 0
            S.op('dve', I("tensor_tensor", OCH[:, lo:NTOK], A_[:, lo:NTOK], XC[:, lo:NTOK], ALU.mult), reads=[bA, bXC], writes=[bOCH])
            S.dma('sp', osc[4 + c], OCH[:], reads=[bOCH], writes=[bOSC[4 + c]])

        for c in range(2):
            su = load_wm(14 + c)
            sv = load_wm(16 + c)
            for bi in blks_out:
                c0, n, t = BLOCKS[bi]
                pb = proj(su, bi)
                S.op('act', I("activation", out=BX[:, c0:c0 + n], in_=PS[pb][:, :n], func=AF.Gelu_apprx_tanh),
                     reads=[bPS[pb]], writes=[bBX])
            for bi in blks_out:
                c0, n, t = BLOCKS[bi]
                nt = n // P
                pv_ = psn[0] % 4
                psn[0] += 1
                fns = []
                for tt in range(nt):
                    for k in range(KD):
                        fns.append(I("matmul", PS[pv_][:, tt * P:(tt + 1) * P], HT[:, k, c0 + tt * P:c0 + (tt + 1) * P], WM[:, sv, k * P:(k + 1) * P],
                            start=(k == 0), stop=(k == KD - 1)))
                S.op('pe', fns, reads=[bHT[bi], bWM[sv]], writes=[bPS[pv_]])
                S.op('act', I("activation", out=VT[:, :n], in_=PS[pv_][:, :n], func=AF.Gelu_apprx_tanh),
                     reads=[bPS[pv_]], writes=[bVT])
                ps_ = psn[0] % 4
                psn[0] += 1
                fns = []
                for tt in range(nt):
                    for hh in range(2):
                        fns.append(I("matmul", PS[ps_][hh * 64:(hh + 1) * 64, tt * P:(tt + 1) * P], VT[:, tt * P + hh * 64:tt * P + (hh + 1) * 64],
                            WST[:, 2 * c + hh, :], start=True, stop=True))
                S.op('pe', fns, reads=[bVT, bWST], writes=[bPS[ps_]])
                ts = bi % 2
                for tt in range(nt):
                    S.op('dve', I("tensor_tensor", TMP[:, ts, tt * P:(tt + 1) * P], PS[ps_][:, tt * P:(tt + 1) * P],
                                                                                BSB[:, c, :], ALU.add),
                         reads=[bPS[ps_], bBSB], writes=[bTMP[ts]])
                S.op('dve', I("tensor_tensor", OCH[:, c0:c0 + n], TMP[:, ts, :n], BX[:, c0:c0 + n], ALU.mult),
                     reads=[bTMP[ts], bBX], writes=[bOCH])
            S.dma('sp', osc[6 + c], OCH[:], reads=[bOCH], writes=[bOSC[6 + c]])

        (bOBLK, bYT2, bWMO) = new_phase(["oblk", "yt2", "wmo"])
        S.dma('sp', WMO[:], wmo_b[l].rearrange("(j p) n -> p j n", p=P), reads=[b_wmo[l]], writes=[bWMO])
        for bi in blks_out:
            c0, n, t = BLOCKS[bi]
            S.dma('sp', OBLK[:, :, :n], osc[:, :, c0:c0 + n].rearrange("k p n -> p k n"), reads=bOSC, writes=[bOBLK])
            for j in range(KD):
                yp = 4 + j % 2
                S.op('pe', [I("matmul", PS[yp][:, :n], WMO[:, j, kc * P:(kc + 1) * P], OBLK[:, kc, :n],
                                                                 start=(kc == 0), stop=(kc == KD - 1)) for kc in range(KD)],
                     reads=[bWMO, bOBLK], writes=[bPS[yp]])
                S.op('act', I("activation", out=YT2[:, j, :n], in_=PS[yp][:, :n], func=AF.Copy),
                     reads=[bPS[yp]], writes=[bYT2])
                S.op('dve', I("tensor_tensor", SQ[:, j, :n], YT2[:, j, :n], YT2[:, j, :n], ALU.mult),
                     reads=[bYT2], writes=[bSQ])
            postnorm_update(l, s, bi, bYT2, YT2)

    stage = 0
    for l in range(L):
        last = (l == L - 1)
        if stage < nstage:
            if l == 0:
                cast_mix(0)
                cast_ffn(0, 1)
            ffn(l, 0)
        stage += 1
        if stage < nstage:
            if l == 0:
                cast_ffn(1, 0)
                cast_mix(1)
                cast_ffn(1, 1)
            mixer(l)
        stage += 1
        if stage < nstage:
            ffn(l, 2, skip_ctx=last)
        stage += 1

    (bO0, bO1) = new_phase(["o0", "o1"])
    bO = [bO0, bO1]
    OUTS = XIN
    ntiles = NTOK // P if debug_ctx else SEQ // P
    for i in range(ntiles):
        tt = (i + 2) if not debug_ctx else i
        blk = 0 if tt < 2 else 1 + (tt - 2) // 4
        slot = i % 2
        for half in range(2):
            pb = (i * 2 + half) % 4
            S.op('pe', [I("transpose", PS[pb][:, k * P:(k + 1) * P], XT[:, half * 4 + k, tt * P:(tt + 1) * P], ID[:]) for k in range(4)],
                 reads=[bXT[blk], bID], writes=[bPS[pb]])
            if half == 0:
                S.op('dve', I("tensor_copy", OUTS[:, slot, 0:512], PS[pb][:]), reads=[bPS[pb]], writes=[bO[slot]])
            else:
                S.op('act', I("activation", out=OUTS[:, slot, 512:1024], in_=PS[pb][:], func=AF.Copy),
                     reads=[bPS[pb]], writes=[bO[slot]])
        if tt < 2:
            dst = outc_d[tt * P:(tt + 1) * P, :]
        else:
            dst = out_d[(tt - 2) * P:(tt - 1) * P, :]
        S.dma('sp', dst, OUTS[:, slot, :], reads=[bO[slot]], is_output=True)

    S.emit()
    st.close()
    return nc


def prep_shared(inp):
    f32 = np.float32
    w_in = np.asarray(inp["ffn_w_in"], f32)
    a = w_in.reshape(L, 2, 8, P, 2, NF, P)
    win_r = np.ascontiguousarray(a.transpose(0, 1, 5, 3, 2, 4, 6)).reshape(L, 2, NF * P, 2048)
    w_out = np.asarray(inp["ffn_w_out"], f32)
    b = w_out.reshape(L, 2, NF, P, KD, P)
    wout_r = np.ascontiguousarray(b.transpose(0, 1, 4, 3, 2, 5)).reshape(L, 2, KD * P, DFF)
    wmi = np.asarray(inp["w_mix_in"], f32).reshape(L, KD, P, NM, P)
    wmi_r = np.ascontiguousarray(wmi.transpose(0, 3, 2, 1, 4)).reshape(L, NM * P, 1024)
    wmo = np.asarray(inp["w_mix_out"], f32).reshape(L, KD, P, KD, P)
    wmo_r = np.ascontiguousarray(wmo.transpose(0, 3, 2, 1, 4)).reshape(L, KD * P, 1024)
    w_r = np.asarray(inp["lru_w_r"], f32)
    w_i = np.asarray(inp["lru_w_i"], f32)
    bd = np.zeros((L, P, 16, P), f32)
    for l in range(L):
        for d in range(2):
            for g, W in enumerate((w_r, w_i)):
                for c in range(4):
                    idx = (d * 2 + g) * 4 + c
                    bd[l, 0:64, idx, 0:64] = W[l, d, 2 * c]
                    bd[l, 64:128, idx, 64:128] = W[l, d, 2 * c + 1]
    bd = bd.reshape(L, P, 2048)
    wst = np.ascontiguousarray(np.asarray(inp["cm_w_s"], f32).transpose(0, 3, 1, 2)).reshape(L, P, 512)
    bs = np.asarray(inp["cm_b_s"], f32)
    bsb = np.zeros((L, P, 2, P), f32)
    for l in range(L):
        for c in range(2):
            bsb[l, 0:64, c, :] = bs[l, 2 * c][None, :]
            bsb[l, 64:128, c, :] = bs[l, 2 * c + 1][None, :]
    bsb = bsb.reshape(L, P, 256)

    pv = np.zeros((P, NPV), f32)

    def put(name, l, arr):
        o = l * PV_PER_LAYER + PV_OFF[name]
        arr = np.asarray(arr, f32)
        pv[:, o:o + arr.shape[0]] = arr.T

    for l in range(L):
        put("bmod", l, np.asarray(inp["b_mod"], f32)[l].reshape(72, P))
        put("npre", l, np.asarray(inp["norm_pre"], f32)[l].reshape(24, P))
        put("npost", l, np.asarray(inp["norm_post"], f32)[l].reshape(24, P))
        lcw = np.asarray(inp["lru_conv_w"], f32)[l]
        put("lcw", l, lcw.reshape(4, 4, P).transpose(1, 0, 2).reshape(16, P))
        put("lcb", l, np.asarray(inp["lru_conv_b"], f32)[l].reshape(4, P))
        put("br", l, np.asarray(inp["lru_b_r"], f32)[l].reshape(8, P))
        put("bi", l, np.asarray(inp["lru_b_i"], f32)[l].reshape(8, P))
        put("lam", l, np.asarray(inp["lru_lambda"], f32)[l].reshape(8, P))
        scw = np.asarray(inp["sc_conv_w"], f32)[l]
        put("scw", l, scw.reshape(3, 2, P).transpose(1, 0, 2).reshape(6, P))
        put("scb", l, np.asarray(inp["sc_conv_b"], f32)[l].reshape(2, P))
    return dict(ident=np.eye(P, dtype=f32), pv=pv, wmod=np.ascontiguousarray(np.asarray(inp["w_mod"], f32)),
                win=win_r, wout=wout_r, wmi=wmi_r, wmo=wmo_r, bd=bd, wst=wst, bsb=bsb)


def kernel(**inp):
    nstage = int(os.environ.get("MK_NSTAGE", "6"))
    debug_ctx = os.environ.get("MK_DEBUG_CTX", "0") == "1"
    import time as _time0
    _tp = _time0.time()
    shared = prep_shared(inp)
    if os.environ.get("MK_VERBOSE"):
        print("prep %.1fs" % (_time0.time() - _tp), flush=True)
    x = np.asarray(inp["x"], np.float32)
    ctx = np.asarray(inp["ctx"], np.float32)
    c = np.asarray(inp["c"], np.float32)
    c_ctx = np.asarray(inp["c_ctx"], np.float32)
    in_maps = []
    for b in range(8):
        cv = np.zeros((P, KD, 2), np.float32)
        cv[:, :, 0] = c[b].reshape(KD, P).T
        cv[:, :, 1] = c_ctx.reshape(KD, P).T
        m = dict(shared)
        m["x"] = np.ascontiguousarray(x[b])
        m["ctx"] = np.ascontiguousarray(ctx[b])
        m["cv"] = cv.reshape(P, 16)
        in_maps.append(m)
    import time as _time
    _t0 = _time.time()
    nc = build(nstage, debug_ctx)
    _t1 = _time.time()
    res = run_bass_kernel_spmd(nc, in_maps, core_ids=list(range(8)))
    if os.environ.get("MK_VERBOSE"):
        print("build %.1fs run %.1fs" % (_t1 - _t0, _time.time() - _t1), flush=True)
    out = np.stack([np.asarray(r["out"], np.float32) for r in res.results], axis=0)
    if debug_ctx:
        kernel.last_ctx = np.stack([np.asarray(r["outc"], np.float32) for r in res.results], axis=0)
    return out
```

```python
import os
import contextlib
import numpy as np
import concourse.bass as bass
import concourse.mybir as mybir
from concourse.bass_utils import run_bass_kernel_spmd

F32 = mybir.dt.float32
BF16 = mybir.dt.bfloat16
AF = mybir.ActivationFunctionType
ALU = mybir.AluOpType

ENGS = ("pe", "act", "dve", "pool", "sp")


def I(_name, *args, **kwargs):
    return lambda e: getattr(e, _name)(*args, **kwargs)


class Buf:
    __slots__ = ("name", "w", "r")

    def __init__(self, name=""):
        self.name = name
        self.w = None
        self.r = []

    def handles(self):
        hs = list(self.r)
        if self.w is not None:
            hs.append(self.w)
        return hs


class Sched:
    def __init__(self, nc, n_dma_sems=32):
        self.nc = nc
        self.ops = {e: [] for e in ENGS}
        self.cnt = {e: 0 for e in ENGS}
        self.waited = {e: {} for e in ENGS}
        self.n_dma_sems = n_dma_sems
        self.dma_next = 0
        self.dma_val = [0] * n_dma_sems
        self.out_dma_handles = []

    def _add_wait(self, eng, h, waits):
        if h is None:
            return
        if h[0] == 'E' and h[1] == eng and eng == 'pe':
            return
        k = (h[0], h[1])
        if self.waited[eng].get(k, 0) >= h[2]:
            return
        self.waited[eng][k] = h[2]
        waits.append(h)

    def _deps(self, eng, reads, writes, extra):
        waits = []
        for b in reads:
            self._add_wait(eng, b.w, waits)
        for b in writes:
            self._add_wait(eng, b.w, waits)
            for r in b.r:
                self._add_wait(eng, r, waits)
        for h in extra:
            self._add_wait(eng, h, waits)
        return waits

    def _commit(self, h, reads, writes):
        for b in reads:
            b.r.append(h)
            if len(b.r) > 64:
                b.r = b.r[-64:]
        for b in writes:
            b.w = h
            b.r = []

    def op(self, eng, fns, reads=(), writes=(), extra=()):
        if not isinstance(fns, (list, tuple)):
            fns = [fns]
        waits = self._deps(eng, reads, writes, extra)
        self.cnt[eng] += 1
        h = ('E', eng, self.cnt[eng])
        self.ops[eng].append((waits, list(fns), ('E', None)))
        self._commit(h, reads, writes)
        return h

    def dma(self, eng, out_ap, in_ap, reads=(), writes=(), extra=(), is_output=False):
        waits = self._deps(eng, reads, writes, extra)
        idx = self.dma_next
        self.dma_next = (self.dma_next + 1) % self.n_dma_sems
        if self.dma_val[idx] > 0:
            self._add_wait(eng, ('D', idx, self.dma_val[idx]), waits)
        self.dma_val[idx] += 16
        h = ('D', idx, self.dma_val[idx])
        fn = I("dma_start", out=out_ap, in_=in_ap)
        self.ops[eng].append((waits, [fn], ('D', idx)))
        self._commit(h, reads, writes)
        if is_output:
            self.out_dma_handles.append(h)
        return h

    def emit(self):
        nc = self.nc
        with contextlib.ExitStack() as st:
            esem = {e: st.enter_context(nc.semaphore("s_" + e)) for e in ENGS}
            dsem = [st.enter_context(nc.semaphore("d_%d" % i)) for i in range(self.n_dma_sems)]
            block = st.enter_context(nc.Block())

            def semof(h):
                return esem[h[1]] if h[0] == 'E' else dsem[h[1]]

            def run(eng_name, eng):
                for waits, fns, inc in self.ops[eng_name]:
                    for h in waits:
                        eng.wait_ge(semof(h), h[2])
                    ins = None
                    for fn in fns:
                        ins = fn(eng)
                    if inc[0] == 'E':
                        ins.then_inc(esem[eng_name], 1)
                    else:
                        ins.then_inc(dsem[inc[1]], 16)
                if eng_name == 'sp':
                    for h in self.out_dma_handles:
                        eng.wait_ge(semof(h), h[2])

            @block.tensor
            def _(e):
                run('pe', e)

            @block.scalar
            def _(e):
                run('act', e)

            @block.vector
            def _(e):
                run('dve', e)

            @block.gpsimd
            def _(e):
                run('pool', e)

            @block.sync
            def _(e):
                run('sp', e)


P = 128
D = 1024
KD = 8
SEQ = 2048
CTX = 256
NTOK = SEQ + CTX
DFF = 2816
NF = 22
NM = 18
L = 2
EPS = 1e-6
BLOCKS = [(0, 256, 1)] + [(256 + 512 * i, 512, 0) for i in range(4)]
NBLK = len(BLOCKS)

PV_LAYOUT = [("bmod", 72), ("npre", 24), ("npost", 24), ("lcw", 16), ("lcb", 4), ("br", 8), ("bi", 8),
             ("lam", 8), ("scw", 6), ("scb", 2)]
PV_OFF = {}
_o = 0
for _n, _w in PV_LAYOUT:
    PV_OFF[_n] = _o
    _o += _w
PV_PER_LAYER = _o
NPV = PV_PER_LAYER * L

OC = 2
OL = 261
XPW = 2312


def xp_off(c0):
    return OC + c0 if c0 < CTX else OL + (c0 - CTX)


def build(nstage=6, debug_ctx=False):
    nc = bass.Bass("TRN2", target_bir_lowering=False)
    S = Sched(nc)
    st = contextlib.ExitStack()

    def din(name, shape, dt=F32):
        return nc.dram_tensor(name, list(shape), dt, kind="ExternalInput").ap()

    def dscr(name, shape, dt=BF16):
        return nc.dram_tensor(name, list(shape), dt, kind="Internal").ap()

    x_d = din("x", [SEQ, D])
    ctx_d = din("ctx", [CTX, D])
    out_d = nc.dram_tensor("out", [SEQ, D], F32, kind="ExternalOutput").ap()
    if debug_ctx:
        outc_d = nc.dram_tensor("outc", [CTX, D], F32, kind="ExternalOutput").ap()
    ident_d = din("ident", [P, P])
    cv_d = din("cv", [P, 16])
    pv_d = din("pv", [P, NPV])
    wmod_d = din("wmod", [L, D, 9 * D])
    win_d = din("win", [L, 2, NF * P, 2048])
    wout_d = din("wout", [L, 2, KD * P, DFF])
    wmi_d = din("wmi", [L, NM * P, 1024])
    wmo_d = din("wmo", [L, KD * P, 1024])
    bd_d = din("bd", [L, P, 2048])
    wst_d = din("wst", [L, P, 512])
    bsb_d = din("bsb", [L, P, 256])

    win_b = dscr("win_b", [L, 2, NF * P, 2048])
    wout_b = dscr("wout_b", [L, 2, KD * P, DFF])
    wmi_b = dscr("wmi_b", [L, NM * P, 1024])
    wmo_b = dscr("wmo_b", [L, KD * P, 1024])
    osc = dscr("osc", [KD, P, NTOK])

    def sb(name, shape, dt=F32):
        return st.enter_context(nc.sbuf_tensor(name, list(shape), dt))

    XT = sb("XT", [P, KD, NTOK])
    HT = sb("HT", [P, KD, NTOK], BF16)
    SQ = sb("SQ", [P, KD, 512], BF16)
    RSTD = sb("RSTD", [P, 2, 512])
    TMP = sb("TMP", [P, 4, 512])
    WM = sb("WM", [P, 4, 1024], BF16)
    BD = sb("BD", [P, 16, P], BF16)
    WST = sb("WST", [P, 4, P], BF16)
    BSB = sb("BSB", [P, 2, P])
    VT = sb("VT", [P, 512], BF16)
    ID = sb("ID", [P, P])
    ONES = sb("ONES", [P, P], BF16)
    PV = sb("PV", [P, NPV])
    SCV = sb("SCV", [P, KD, 2])
    SCVB = sb("SCVB", [P, KD, 2], BF16)
    MODV = sb("MODV", [P, 72, 2])
    MODP = sb("MODP", [P, L * 3 * 3 * 16])
    CA = sb("CA", [P, 16])
    UW = 14592
    U = sb("U", [P, UW])
    PS = [st.enter_context(nc.psum_tensor("ps%d" % i, [P, 512], F32)) for i in range(8)]
    bPS = [Buf("ps%d" % i) for i in range(8)]

    def uv(off_b, nbytes, dt):
        a = U[:, off_b // 4:(off_b + nbytes) // 4]
        return a.bitcast(BF16) if dt == BF16 else a

    KB = 1024
    ACTT = uv(0, 22 * KB, BF16).rearrange("p (f n) -> p f n", f=NF)
    YT = uv(22 * KB, 16 * KB, F32).rearrange("p (k n) -> p k n", k=KD)
    WIN = uv(38 * KB, 8 * KB, BF16).rearrange("p (s n) -> p s n", s=2)
    WOUT = uv(46 * KB, 11 * KB, BF16).rearrange("p (s n) -> p s n", s=4)
    WIN2 = WM[:].rearrange("p (s a) n -> p s (a n)", s=2)
    XIN = uv(0, 8 * KB, F32).rearrange("p (s n) -> p s n", s=2)
    WMS = uv(8 * KB, 32 * KB, BF16).rearrange("p (s k n) -> p s k n", s=4, k=KD)
    XP = uv(0, XPW * 4, F32)
    o1 = 9248
    XC = uv(o1, 9216, F32)
    A_ = uv(o1 + 9216, 9216, F32)
    BX = uv(o1 + 2 * 9216, 9216, F32)
    YY = uv(o1 + 3 * 9216, 9216, F32)
    XCB = uv(o1 + 4 * 9216, 4608, BF16)
    OCH = uv(o1 + 4 * 9216 + 4608, 4608, BF16)
    YT2 = YT
    OBLK = uv(0, 8 * KB, BF16).rearrange("p (k n) -> p k n", k=KD)
    WMO = uv(38 * KB, 16 * KB, BF16).rearrange("p (j n) -> p j n", j=KD)

    Ubufs = []

    def new_phase(names):
        hs = []
        for b in Ubufs:
            hs.extend(b.handles())
        best = {}
        for h in hs:
            k = (h[0], h[1])
            if k not in best or best[k][2] < h[2]:
                best[k] = h
        del Ubufs[:]
        out = []
        for n in names:
            b = Buf(n)
            b.r = list(best.values())
            Ubufs.append(b)
            out.append(b)
        return out

    bXT = [Buf("xt%d" % i) for i in range(NBLK)]
    bHT = [Buf("ht%d" % i) for i in range(NBLK)]
    bSQ, bRS0, bRS1 = Buf("sq"), Buf("rs0"), Buf("rs1")
    bTMP = [Buf("tmp%d" % i) for i in range(4)]
    bWM = [Buf("wm%d" % i) for i in range(4)]
    bBD, bWST, bBSB, bVT = Buf(), Buf(), Buf(), Buf()
    bID, bONES, bPV, bSCV, bMODV, bMODP, bCA, bSCVB = Buf(), Buf(), Buf(), Buf(), Buf(), Buf(), Buf(), Buf()
    b_win = [[Buf() for s in range(2)] for l in range(L)]
    b_wout = [[Buf() for s in range(2)] for l in range(L)]
    b_wmi = [Buf() for l in range(L)]
    b_wmo = [Buf() for l in range(L)]
    bOSC = [Buf() for c in range(KD)]

    def pvc(name, l, j, n=1):
        o = l * PV_PER_LAYER + PV_OFF[name] + j
        return PV[:, o:o + n]

    def modp(l, s, kind, k, t):
        o = (((l * 3 + s) * 3 + kind) * 8 + k) * 2 + t
        return MODP[:, o:o + 1]

    def cast_ffn(l, s):
        S.dma('pool', win_b[l, s], win_d[l, s], writes=[b_win[l][s]])
        S.dma('pool', wout_b[l, s].rearrange("r (a b) -> r a b", b=1408),
              wout_d[l, s].rearrange("r (a b) -> r a b", b=1408), writes=[b_wout[l][s]])

    def cast_mix(l):
        S.dma('pool', wmi_b[l], wmi_d[l], writes=[b_wmi[l]])
        S.dma('pool', wmo_b[l], wmo_d[l], writes=[b_wmo[l]])

    cast_ffn(0, 0)

    S.dma('sp', ID[:], ident_d, writes=[bID])
    S.dma('sp', PV[:], pv_d, writes=[bPV])
    S.dma('sp', SCV[:].rearrange("p k t -> p (k t)"), cv_d, writes=[bSCV])
    S.op('dve', I("memset", ONES[:], 1.0), writes=[bONES])
    S.op('act', I("activation", out=SCVB[:], in_=SCV[:], func=AF.Silu), reads=[bSCV], writes=[bSCVB])
    for l in range(L):
        S.op('act', I("activation", out=CA[:, l * 8:(l + 1) * 8], in_=pvc("lam", l, 0, 8), func=AF.Exp, scale=-1.0),
             reads=[bPV], writes=[bCA])
    S.op('act', I("activation", out=CA[:], in_=CA[:], func=AF.Ln, bias=1.0), reads=[bCA], writes=[bCA])
    S.op('dve', I("tensor_scalar", CA[:], CA[:], -8.0, None, ALU.mult), reads=[bCA], writes=[bCA])

    (bXIN0, bXIN1, bWMS0, bWMS1, bWMS2, bWMS3) = new_phase(["xin0", "xin1", "wms0", "wms1", "wms2", "wms3"])
    bXIN = [bXIN0, bXIN1]
    bWMS = [bWMS0, bWMS1, bWMS2, bWMS3]
    cpy = 0
    for tt in range(NTOK // P):
        src = ctx_d[tt * P:(tt + 1) * P, :] if tt < 2 else x_d[(tt - 2) * P:(tt - 1) * P, :]
        slot = tt % 2
        blk = 0 if tt < 2 else 1 + (tt - 2) // 4
        S.dma('sp', XIN[:, slot, :], src, writes=[bXIN[slot]])
        for half in range(2):
            pb = (tt * 2 + half) % 4
            S.op('pe', [I("transpose", PS[pb][:, k * P:(k + 1) * P], XIN[:, slot, (half * 4 + k) * P:(half * 4 + k + 1) * P], ID[:]) for k in range(4)],
                 reads=[bXIN[slot], bID], writes=[bPS[pb]])
            dst = XT[:, half * 4:(half + 1) * 4, tt * P:(tt + 1) * P]
            srcp = PS[pb][:].rearrange("p (k t) -> p k t", k=4)
            if cpy % 2 == 0:
                S.op('dve', I("tensor_copy", dst, srcp), reads=[bPS[pb]], writes=[bXT[blk]])
            else:
                S.op('act', I("activation", out=dst, in_=srcp, func=AF.Copy), reads=[bPS[pb]], writes=[bXT[blk]])
            cpy += 1

    MODPS = PS[4][:, 0:144].rearrange("p (m t) -> p m t", t=2)
    SLAB = 512
    for l in range(L):
        nsl = 9 * D // SLAB
        for sl in range(nsl):
            slot = (l * nsl + sl) % 4
            S.dma('pool', WMS[:, slot], wmod_d[l].rearrange("(k p) m -> p k m", p=P)[:, :, sl * SLAB:(sl + 1) * SLAB],
                  writes=[bWMS[slot]])
            fns = []
            for mm in range(SLAB // P):
                m = sl * (SLAB // P) + mm
                for k in range(KD):
                    fns.append(I("matmul", MODPS[:, m, :], WMS[:, slot, k, mm * P:(mm + 1) * P], SCVB[:, k, :], start=(k == 0), stop=(k == KD - 1)))
            S.op('pe', fns, reads=[bWMS[slot], bSCVB], writes=[bPS[4]])
        for t in range(2):
            S.op('dve', I("tensor_tensor", MODV[:, :, t], MODPS[:, :, t], pvc("bmod", l, 0, 72), ALU.add),
                 reads=[bPS[4], bPV], writes=[bMODV])
        for s in range(3):
            wgt = 1.0 if s == 1 else 0.5
            for t in range(2):
                o = (((l * 3 + s) * 3 + 0) * 8) * 2
                dstA = MODP[:, o:o + 16].rearrange("p (k t) -> p k t", t=2)[:, :, t]
                dstS = MODP[:, o + 16:o + 32].rearrange("p (k t) -> p k t", t=2)[:, :, t]
                dstG = MODP[:, o + 32:o + 48].rearrange("p (k t) -> p k t", t=2)[:, :, t]
                sh = MODV[:, (3 * s) * 8:(3 * s) * 8 + 8, t]
                sc = MODV[:, (3 * s + 1) * 8:(3 * s + 1) * 8 + 8, t]
                ga = MODV[:, (3 * s + 2) * 8:(3 * s + 2) * 8 + 8, t]
                S.op('dve', I("scalar_tensor_tensor", dstA, sc, 1.0, pvc("npre", l, s * 8, 8), ALU.add, ALU.mult), reads=[bMODV, bPV], writes=[bMODP])
                S.op('dve', I("tensor_copy", dstS, sh), reads=[bMODV], writes=[bMODP])
                S.op('dve', I("scalar_tensor_tensor", dstG, ga, wgt, pvc("npost", l, s * 8, 8), ALU.mult, ALU.mult), reads=[bMODV, bPV], writes=[bMODP])

    def rstd_from(ssp, bssp, n, slot):
        brs = bRS0 if slot == 0 else bRS1
        S.op('act', I("activation", out=RSTD[:, slot, :n], in_=ssp[:, :n], func=AF.Sqrt, scale=1.0 / D, bias=PVEPS),
             reads=[bssp, bEPS], writes=[brs])
        S.op('dve', I("reciprocal", RSTD[:, slot, :n], RSTD[:, slot, :n]), reads=[brs], writes=[brs])
        return brs

    EPSB = sb("EPSB", [P, 1])
    bEPS = Buf()
    S.op('dve', I("memset", EPSB[:], EPS), writes=[bEPS])
    PVEPS = EPSB[:, 0:1]

    def prenorm_sq(l, s, bi):
        c0, n, t = BLOCKS[bi]
        S.op('act', I("activation", out=SQ[:, :, :n], in_=XT[:, :, c0:c0 + n], func=AF.Square),
             reads=[bXT[bi]], writes=[bSQ])

    def prenorm_rest(l, s, bi):
        c0, n, t = BLOCKS[bi]
        S.op('pe', [I("matmul", PS[6][:, :n], ONES[:], SQ[:, k, :n], start=(k == 0), stop=(k == KD - 1)) for k in range(KD)],
             reads=[bSQ, bONES], writes=[bPS[6]])
        brs = rstd_from(PS[6], bPS[6], n, 0)
        for k in range(KD):
            ts = 2 + (k % 2)
            S.op('dve', I("scalar_tensor_tensor", TMP[:, ts, :n], XT[:, k, c0:c0 + n], modp(l, s, 0, k, t), RSTD[:, 0, :n], ALU.mult, ALU.mult),
                 reads=[bXT[bi], brs, bMODP], writes=[bTMP[ts]])
            S.op('act', I("activation", out=HT[:, k, c0:c0 + n], in_=TMP[:, ts, :n], func=AF.Identity,
                                                          bias=modp(l, s, 1, k, t), scale=1.0),
                 reads=[bTMP[ts], bMODP], writes=[bHT[bi]])

    def postnorm_update(l, s, bi, bYT, YTv):
        c0, n, t = BLOCKS[bi]
        S.op('pe', [I("matmul", PS[7][:, :n], ONES[:], SQ[:, k, :n], start=(k == 0), stop=(k == KD - 1)) for k in range(KD)],
             reads=[bSQ, bONES], writes=[bPS[7]])
        brs = rstd_from(PS[7], bPS[7], n, 1)
        for k in range(KD):
            ts = k % 2
            S.op('dve', I("scalar_tensor_tensor", TMP[:, ts, :n], YTv[:, k, :n], modp(l, s, 2, k, t), RSTD[:, 1, :n], ALU.mult, ALU.mult),
                 reads=[bYT, brs, bMODP], writes=[bTMP[ts]])
            S.op('dve', I("tensor_tensor", XT[:, k, c0:c0 + n], XT[:, k, c0:c0 + n], TMP[:, ts, :n], ALU.add),
                 reads=[bTMP[ts]], writes=[bXT[bi]])

    pending = []

    def after_update(l, s, bi):
        if s < 2:
            ns = (l, s + 1)
        elif l + 1 < L:
            ns = (l + 1, 0)
        else:
            return
        if ns == (L - 1, 2) and BLOCKS[bi][2] == 1:
            return
        pending.append([ns[0], ns[1], bi, 0])

    def drain_sq():
        if pending and pending[0][3] == 0:
            it = pending[0]
            prenorm_sq(it[0], it[1], it[2])
            it[3] = 1

    def drain_rest():
        if pending and pending[0][3] == 1:
            it = pending.pop(0)
            prenorm_rest(it[0], it[1], it[2])

    def drain_all():
        while pending:
            drain_sq()
            drain_rest()

    def drain_until(l, s, bi):
        while any(it[0] == l and it[1] == s and it[2] == bi for it in pending):
            drain_sq()
            drain_rest()

    def ffn(l, s, skip_ctx=False):
        wi = s // 2
        (bACTT, bYT, bWIN0, bWIN1, bWOUT0, bWOUT1, bWOUT2, bWOUT3) = new_phase(["actt", "yt", "win0", "win1", "wout0", "wout1", "wout2", "wout3"])
        bWIN = [bWIN0, bWIN1]
        bWOUT = [bWOUT0, bWOUT1, bWOUT2, bWOUT3]
        blks = [bi for bi in range(NBLK) if not (skip_ctx and BLOCKS[bi][2] == 1)]
        nwin = 0
        nwout = 0
        for ii, bi in enumerate(blks):
            c0, n, t = BLOCKS[bi]
            drain_until(l, s, bi)
            for f in range(NF):
                slot = nwin % 4
                nwin += 1
                WINs = WIN[:, slot, :] if slot < 2 else WIN2[:, slot - 2, :]
                bWs = [bWIN[slot]] if slot < 2 else [bWM[2 * (slot - 2)], bWM[2 * (slot - 2) + 1]]
                S.dma('sp', WINs, win_b[l, wi, f * P:(f + 1) * P, :], reads=[b_win[l][wi]], writes=bWs)
                gp = f % 2
                up = 2 + f % 2
                S.op('pe', [I("matmul", PS[gp][:, :n], WINs[:, k * 256:k * 256 + P], HT[:, k, c0:c0 + n],
                                                                     start=(k == 0), stop=(k == KD - 1)) for k in range(KD)],
                     reads=bWs + [bHT[bi]], writes=[bPS[gp]])
                S.op('pe', [I("matmul", PS[up][:, :n], WINs[:, k * 256 + P:k * 256 + 2 * P], HT[:, k, c0:c0 + n],
                                                                     start=(k == 0), stop=(k == KD - 1)) for k in range(KD)],
                     reads=bWs + [bHT[bi]], writes=[bPS[up]])
                ts = f % 2
                S.op('act', I("activation", out=TMP[:, ts, :n], in_=PS[gp][:, :n], func=AF.Silu),
                     reads=[bPS[gp]], writes=[bTMP[ts]])
                S.op('dve', I("tensor_tensor", ACTT[:, f, :n], TMP[:, ts, :n], PS[up][:, :n], ALU.mult),
                     reads=[bTMP[ts], bPS[up]], writes=[bACTT])
                if f in (3, 13):
                    drain_sq()
                if f in (8, 18):
                    drain_rest()
            if pending and pending[0][3] == 1:
                drain_rest()
            for j in range(KD):
                yp = 4 + j % 2
                for hf in range(2):
                    slot = nwout % 4
                    nwout += 1
                    S.dma('sp', WOUT[:, slot, :], wout_b[l, wi, j * P:(j + 1) * P, hf * 1408:(hf + 1) * 1408], reads=[b_wout[l][wi]], writes=[bWOUT[slot]])
                    S.op('pe', [I("matmul", PS[yp][:, :n], WOUT[:, slot, f * P:(f + 1) * P], ACTT[:, hf * 11 + f, :n],
                                                                         start=(hf == 0 and f == 0), stop=(hf == 1 and f == 10)) for f in range(11)],
                         reads=[bWOUT[slot], bACTT], writes=[bPS[yp]])
                S.op('act', I("activation", out=YT[:, j, :n], in_=PS[yp][:, :n], func=AF.Copy),
                     reads=[bPS[yp]], writes=[bYT])
                S.op('dve', I("tensor_tensor", SQ[:, j, :n], YT[:, j, :n], YT[:, j, :n], ALU.mult),
                     reads=[bYT], writes=[bSQ])
            postnorm_update(l, s, bi, bYT, YT)
            after_update(l, s, bi)

    def mixer(l):
        last = (l == L - 1)
        s = 1
        (bXP, bXC, bA, bBX, bYY, bXCB, bOCH) = new_phase(["xp", "xc", "a", "bx", "yy", "xcb", "och"])
        S.dma('pool', BD[:].rearrange("p a b -> p (a b)"), bd_d[l], writes=[bBD])
        S.dma('pool', WST[:].rearrange("p a b -> p (a b)"), wst_d[l], writes=[bWST])
        S.dma('sp', BSB[:].rearrange("p a b -> p (a b)"), bsb_d[l], writes=[bBSB])
        drain_all()
        S.op('dve', I("memset", XP[:], 0.0), writes=[bXP])
        wm_n = [0]

        def load_wm(m):
            slot = wm_n[0] % 4
            wm_n[0] += 1
            S.dma('sp', WM[:, slot, :], wmi_b[l, m * P:(m + 1) * P, :], reads=[b_wmi[l]], writes=[bWM[slot]])
            return slot

        psn = [0]

        def proj(slot, bi):
            c0, n, t = BLOCKS[bi]
            pb = psn[0] % 4
            psn[0] += 1
            S.op('pe', [I("matmul", PS[pb][:, :n], WM[:, slot, k * P:(k + 1) * P], HT[:, k, c0:c0 + n],
                                                start=(k == 0), stop=(k == KD - 1)) for k in range(KD)],
                 reads=[bWM[slot], bHT[bi]], writes=[bPS[pb]])
            return pb

        blks_all = list(range(NBLK))
        blks_out = [bi for bi in blks_all if not (last and BLOCKS[bi][2] == 1)]
        segs = [(0, CTX, OC), (CTX, SEQ, OL)]

        for c in range(4):
            sx = load_wm(c)
            sg = load_wm(4 + c)
            for bi in blks_all:
                c0, n, t = BLOCKS[bi]
                pb = proj(sx, bi)
                S.op('act', I("activation", out=XP[:, xp_off(c0):xp_off(c0) + n], in_=PS[pb][:, :n], func=AF.Copy),
                     reads=[bPS[pb]], writes=[bXP])
            for (t0, n, o) in segs:
                S.op('dve', I("tensor_scalar", XC[:, t0:t0 + n], XP[:, o - 2:o - 2 + n], pvc("lcw", l, c * 4 + 0),
                                                                      pvc("lcb", l, c), ALU.mult, ALU.add),
                     reads=[bXP, bPV], writes=[bXC])
                for tap in range(1, 4):
                    S.op('dve', I("scalar_tensor_tensor", XC[:, t0:t0 + n], XP[:, o - 2 + tap:o - 2 + tap + n], pvc("lcw", l, c * 4 + tap), XC[:, t0:t0 + n], ALU.mult, ALU.add),
                         reads=[bXP, bPV], writes=[bXC])
            S.op('act', I("activation", out=XCB[:], in_=XC[:], func=AF.Copy), reads=[bXC], writes=[bXCB])
            T3 = XP[:, 0:NTOK]
            for d in range(2):
                for bi in blks_all:
                    c0, n, t = BLOCKS[bi]
                    pr = psn[0] % 4
                    psn[0] += 1
                    S.op('pe', I("matmul", PS[pr][:, :n], BD[:, (d * 2 + 0) * 4 + c, :], XCB[:, c0:c0 + n], start=True, stop=True),
                         reads=[bBD, bXCB], writes=[bPS[pr]])
                    S.op('act', I("activation", out=A_[:, c0:c0 + n], in_=PS[pr][:, :n], func=AF.Sigmoid,
                                                                             bias=pvc("br", l, d * 4 + c), scale=1.0),
                         reads=[bPS[pr], bPV], writes=[bA])
                    pi = psn[0] % 4
                    psn[0] += 1
                    S.op('pe', I("matmul", PS[pi][:, :n], BD[:, (d * 2 + 1) * 4 + c, :], XCB[:, c0:c0 + n], start=True, stop=True),
                         reads=[bBD, bXCB], writes=[bPS[pi]])
                    S.op('act', I("activation", out=BX[:, c0:c0 + n], in_=PS[pi][:, :n], func=AF.Sigmoid,
                                                                             bias=pvc("bi", l, d * 4 + c), scale=1.0),
                         reads=[bPS[pi], bPV], writes=[bBX])
                cao = l * 8 + d * 4 + c
                S.op('act', I("activation", out=A_[:], in_=A_[:], func=AF.Exp, scale=CA[:, cao:cao + 1]),
                     reads=[bA, bCA], writes=[bA])
                S.op('dve', I("tensor_tensor", BX[:], BX[:], XC[:], ALU.mult), reads=[bBX, bXC], writes=[bBX])
                S.op('act', I("activation", out=T3, in_=A_[:], func=AF.Square), reads=[bA], writes=[bXP])
                S.op('act', I("activation", out=T3, in_=T3, func=AF.Sqrt, scale=-1.0, bias=1.0), reads=[bXP], writes=[bXP])
                S.op('dve', I("tensor_tensor", BX[:], BX[:], T3, ALU.mult), reads=[bBX, bXP], writes=[bBX])
                Hd = YY if d == 0 else XP[:, 0:NTOK]
                bH = bYY if d == 0 else bXP
                if d == 0:
                    S.op('dve', I("tensor_tensor_scan", YY[:, 0:CTX], A_[:, 0:CTX], BX[:, 0:CTX], 0.0, ALU.mult, ALU.add),
                         reads=[bA, bBX], writes=[bYY])
                    S.op('dve', I("tensor_tensor_scan", YY[:, CTX:NTOK], A_[:, CTX:NTOK], BX[:, CTX:NTOK], YY[:, CTX - 1:CTX], ALU.mult, ALU.add),
                         reads=[bA, bBX, bYY], writes=[bYY])
                else:
                    S.op('dve', I("tensor_tensor_scan", T3[:, 0:CTX][:, ::-1], A_[:, 0:CTX][:, ::-1], BX[:, 0:CTX][:, ::-1], 0.0, ALU.mult, ALU.add),
                         reads=[bA, bBX], writes=[bXP])
                    S.op('dve', I("tensor_tensor_scan", T3[:, CTX:NTOK][:, ::-1], A_[:, CTX:NTOK][:, ::-1], BX[:, CTX:NTOK][:, ::-1], T3[:, 0:1],
                                                               ALU.mult, ALU.add),
                         reads=[bA, bBX, bXP], writes=[bXP])
                    S.op('dve', I("tensor_tensor", YY[:], YY[:], T3, ALU.add), reads=[bXP], writes=[bYY])
            S.op('dve', I("memset", XP[:], 0.0), writes=[bXP])
            for bi in blks_out:
                c0, n, t = BLOCKS[bi]
                pb = proj(sg, bi)
                ts = bi % 2
                S.op('act', I("activation", out=TMP[:, ts, :n], in_=PS[pb][:, :n], func=AF.Gelu_apprx_tanh),
                     reads=[bPS[pb]], writes=[bTMP[ts]])
                S.op('dve', I("tensor_tensor", OCH[:, c0:c0 + n], TMP[:, ts, :n], YY[:, c0:c0 + n], ALU.mult),
                     reads=[bTMP[ts], bYY], writes=[bOCH])
            S.dma('sp', osc[c], OCH[:], reads=[bOCH], writes=[bOSC[c]])

        for c in range(2):
            sbg = load_wm(8 + c)
            scg = load_wm(10 + c)
            sxx = load_wm(12 + c)
            for bi in blks_out:
                c0, n, t = BLOCKS[bi]
                p1 = proj(sxx, bi)
                ts = bi % 2
                S.op('act', I("activation", out=TMP[:, ts, :n], in_=PS[p1][:, :n], func=AF.Copy),
                     reads=[bPS[p1]], writes=[bTMP[ts]])
                p2 = proj(scg, bi)
                S.op('dve', I("tensor_tensor", XP[:, xp_off(c0):xp_off(c0) + n], TMP[:, ts, :n], PS[p2][:, :n], ALU.mult),
                     reads=[bTMP[ts], bPS[p2]], writes=[bXP])
                p3 = proj(sbg, bi)
                S.op('act', I("activation", out=A_[:, c0:c0 + n], in_=PS[p3][:, :n], func=AF.Copy),
                     reads=[bPS[p3]], writes=[bA])
            w0, w1, w2, bb = pvc("scw", l, c * 3 + 0), pvc("scw", l, c * 3 + 1), pvc("scw", l, c * 3 + 2), pvc("scb", l, c)
            if not last:
                S.op('dve', I("tensor_scalar", XC[:, 0:CTX], XP[:, OC:OC + CTX], w1, bb, ALU.mult, ALU.add), reads=[bXP, bPV], writes=[bXC])
                S.op('dve', I("scalar_tensor_tensor", XC[:, 0:CTX], XP[:, OC - 1:OC - 1 + CTX], w0, XC[:, 0:CTX], ALU.mult, ALU.add),
                     reads=[bXP, bPV], writes=[bXC])
                S.op('dve', I("scalar_tensor_tensor", XC[:, 0:CTX], XP[:, OC + 1:OC + 1 + CTX], w2, XC[:, 0:CTX], ALU.mult, ALU.add),
                     reads=[bXP, bPV], writes=[bXC])
            PRL = XP[:, OL:OL + SEQ].rearrange("p (r w) -> p r w", w=64)
            CVL = XC[:, CTX:NTOK].rearrange("p (r w) -> p r w", w=64)
            S.op('dve', I("tensor_scalar", XC[:, CTX:NTOK], XP[:, OL:OL + SEQ], w1, bb, ALU.mult, ALU.add), reads=[bXP, bPV], writes=[bXC])
            S.op('dve', I("scalar_tensor_tensor", CVL[:, :, 1:64], PRL[:, :, 0:63], w0, CVL[:, :, 1:64], ALU.mult, ALU.add),
                 reads=[bXP, bPV], writes=[bXC])
            S.op('dve', I("scalar_tensor_tensor", CVL[:, :, 0:63], PRL[:, :, 1:64], w2, CVL[:, :, 0:63], ALU.mult, ALU.add),
                 reads=[bXP, bPV], writes=[bXC])
            lo = CTX if last else 0
            S.op('dve', I("tensor_tensor", OCH[:, lo:NTOK], A_[:, lo:NTOK], XC[:, lo:NTOK], ALU.mult), reads=[bA, bXC], writes=[bOCH])
            S.dma('sp', osc[4 + c], OCH[:], reads=[bOCH], writes=[bOSC[4 + c]])

        for c in range(2):
            su = load_wm(14 + c)
            sv = load_wm(16 + c)
            for bi in blks_out:
                c0, n, t = BLOCKS[bi]
                pb = proj(su, bi)
                S.op('act', I("activation", out=BX[:, c0:c0 + n], in_=PS[pb][:, :n], func=AF.Gelu_apprx_tanh),
                     reads=[bPS[pb]], writes=[bBX])
            for bi in blks_out:
                c0, n, t = BLOCKS[bi]
                nt = n // P
                pv_ = psn[0] % 4
                psn[0] += 1
                fns = []
                for tt in range(nt):
                    for k in range(KD):
                        fns.append(I("matmul", PS[pv_][:, tt * P:(tt + 1) * P], HT[:, k, c0 + tt * P:c0 + (tt + 1) * P], WM[:, sv, k * P:(k + 1) * P],
                            start=(k == 0), stop=(k == KD - 1)))
                S.op('pe', fns, reads=[bHT[bi], bWM[sv]], writes=[bPS[pv_]])
                S.op('act', I("activation", out=VT[:, :n], in_=PS[pv_][:, :n], func=AF.Gelu_apprx_tanh),
                     reads=[bPS[pv_]], writes=[bVT])
                ps_ = psn[0] % 4
                psn[0] += 1
                fns = []
                for tt in range(nt):
                    for hh in range(2):
                        fns.append(I("matmul", PS[ps_][hh * 64:(hh + 1) * 64, tt * P:(tt + 1) * P], VT[:, tt * P + hh * 64:tt * P + (hh + 1) * 64],
                            WST[:, 2 * c + hh, :], start=True, stop=True))
                S.op('pe', fns, reads=[bVT, bWST], writes=[bPS[ps_]])
                ts = bi % 2
                for tt in range(nt):
                    S.op('dve', I("tensor_tensor", TMP[:, ts, tt * P:(tt + 1) * P], PS[ps_][:, tt * P:(tt + 1) * P],
                                                                                BSB[:, c, :], ALU.add),
                         reads=[bPS[ps_], bBSB], writes=[bTMP[ts]])
                S.op('dve', I("tensor_tensor", OCH[:, c0:c0 + n], TMP[:, ts, :n], BX[:, c0:c0 + n], ALU.mult),
                     reads=[bTMP[ts], bBX], writes=[bOCH])
            S.dma('sp', osc[6 + c], OCH[:], reads=[bOCH], writes=[bOSC[6 + c]])

        (bOBLK, bYT2, bWMO) = new_phase(["oblk", "yt2", "wmo"])
        S.dma('sp', WMO[:], wmo_b[l].rearrange("(j p) n -> p j n", p=P), reads=[b_wmo[l]], writes=[bWMO])
        for bi in blks_out:
            c0, n, t = BLOCKS[bi]
            S.dma('sp', OBLK[:, :, :n], osc[:, :, c0:c0 + n].rearrange("k p n -> p k n"), reads=bOSC, writes=[bOBLK])
            for j in range(KD):
                yp = 4 + j % 2
                S.op('pe', [I("matmul", PS[yp][:, :n], WMO[:, j, kc * P:(kc + 1) * P], OBLK[:, kc, :n],
                                                                 start=(kc == 0), stop=(kc == KD - 1)) for kc in range(KD)],
                     reads=[bWMO, bOBLK], writes=[bPS[yp]])
                S.op('act', I("activation", out=YT2[:, j, :n], in_=PS[yp][:, :n], func=AF.Copy),
                     reads=[bPS[yp]], writes=[bYT2])
                S.op('dve', I("tensor_tensor", SQ[:, j, :n], YT2[:, j, :n], YT2[:, j, :n], ALU.mult),
                     reads=[bYT2], writes=[bSQ])
            postnorm_update(l, s, bi, bYT2, YT2)
            after_update(l, s, bi)
            drain_all()

    for bi in range(NBLK):
        pending.append([0, 0, bi, 0])
    stage = 0
    for l in range(L):
        last = (l == L - 1)
        if stage < nstage:
            if l == 0:
                cast_mix(0)
                cast_ffn(0, 1)
            ffn(l, 0)
        stage += 1
        if stage < nstage:
            if l == 0:
                cast_ffn(1, 0)
                cast_mix(1)
                cast_ffn(1, 1)
            mixer(l)
        stage += 1
        if stage < nstage:
            ffn(l, 2, skip_ctx=last)
        stage += 1

    (bO0, bO1) = new_phase(["o0", "o1"])
    bO = [bO0, bO1]
    OUTS = XIN
    ntiles = NTOK // P if debug_ctx else SEQ // P
    for i in range(ntiles):
        tt = (i + 2) if not debug_ctx else i
        blk = 0 if tt < 2 else 1 + (tt - 2) // 4
        slot = i % 2
        for half in range(2):
            pb = (i * 2 + half) % 4
            S.op('pe', [I("transpose", PS[pb][:, k * P:(k + 1) * P], XT[:, half * 4 + k, tt * P:(tt + 1) * P], ID[:]) for k in range(4)],
                 reads=[bXT[blk], bID], writes=[bPS[pb]])
            if half == 0:
                S.op('dve', I("tensor_copy", OUTS[:, slot, 0:512], PS[pb][:]), reads=[bPS[pb]], writes=[bO[slot]])
            else:
                S.op('act', I("activation", out=OUTS[:, slot, 512:1024], in_=PS[pb][:], func=AF.Copy),
                     reads=[bPS[pb]], writes=[bO[slot]])
        if tt < 2:
            dst = outc_d[tt * P:(tt + 1) * P, :]
        else:
            dst = out_d[(tt - 2) * P:(tt - 1) * P, :]
        S.dma('sp', dst, OUTS[:, slot, :], reads=[bO[slot]], is_output=True)

    S.emit()
    st.close()
    return nc


def prep_shared(inp):
    f32 = np.float32
    w_in = np.asarray(inp["ffn_w_in"], f32)
    a = w_in.reshape(L, 2, 8, P, 2, NF, P)
    win_r = np.ascontiguousarray(a.transpose(0, 1, 5, 3, 2, 4, 6)).reshape(L, 2, NF * P, 2048)
    w_out = np.asarray(inp["ffn_w_out"], f32)
    b = w_out.reshape(L, 2, NF, P, KD, P)
    wout_r = np.ascontiguousarray(b.transpose(0, 1, 4, 3, 2, 5)).reshape(L, 2, KD * P, DFF)
    wmi = np.asarray(inp["w_mix_in"], f32).reshape(L, KD, P, NM, P)
    wmi_r = np.ascontiguousarray(wmi.transpose(0, 3, 2, 1, 4)).reshape(L, NM * P, 1024)
    wmo = np.asarray(inp["w_mix_out"], f32).reshape(L, KD, P, KD, P)
    wmo_r = np.ascontiguousarray(wmo.transpose(0, 3, 2, 1, 4)).reshape(L, KD * P, 1024)
    w_r = np.asarray(inp["lru_w_r"], f32)
    w_i = np.asarray(inp["lru_w_i"], f32)
    bd = np.zeros((L, P, 16, P), f32)
    for l in range(L):
        for d in range(2):
            for g, W in enumerate((w_r, w_i)):
                for c in range(4):
                    idx = (d * 2 + g) * 4 + c
                    bd[l, 0:64, idx, 0:64] = W[l, d, 2 * c]
                    bd[l, 64:128, idx, 64:128] = W[l, d, 2 * c + 1]
    bd = bd.reshape(L, P, 2048)
    wst = np.ascontiguousarray(np.asarray(inp["cm_w_s"], f32).transpose(0, 3, 1, 2)).reshape(L, P, 512)
    bs = np.asarray(inp["cm_b_s"], f32)
    bsb = np.zeros((L, P, 2, P), f32)
    for l in range(L):
        for c in range(2):
            bsb[l, 0:64, c, :] = bs[l, 2 * c][None, :]
            bsb[l, 64:128, c, :] = bs[l, 2 * c + 1][None, :]
    bsb = bsb.reshape(L, P, 256)

    pv = np.zeros((P, NPV), f32)

    def put(name, l, arr):
        o = l * PV_PER_LAYER + PV_OFF[name]
        arr = np.asarray(arr, f32)
        pv[:, o:o + arr.shape[0]] = arr.T

    for l in range(L):
        put("bmod", l, np.asarray(inp["b_mod"], f32)[l].reshape(72, P))
        put("npre", l, np.asarray(inp["norm_pre"], f32)[l].reshape(24, P))
        put("npost", l, np.asarray(inp["norm_post"], f32)[l].reshape(24, P))
        lcw = np.asarray(inp["lru_conv_w"], f32)[l]
        put("lcw", l, lcw.reshape(4, 4, P).transpose(1, 0, 2).reshape(16, P))
        put("lcb", l, np.asarray(inp["lru_conv_b"], f32)[l].reshape(4, P))
        put("br", l, np.asarray(inp["lru_b_r"], f32)[l].reshape(8, P))
        put("bi", l, np.asarray(inp["lru_b_i"], f32)[l].reshape(8, P))
        put("lam", l, np.asarray(inp["lru_lambda"], f32)[l].reshape(8, P))
        scw = np.asarray(inp["sc_conv_w"], f32)[l]
        put("scw", l, scw.reshape(3, 2, P).transpose(1, 0, 2).reshape(6, P))
        put("scb", l, np.asarray(inp["sc_conv_b"], f32)[l].reshape(2, P))
    return dict(ident=np.eye(P, dtype=f32), pv=pv, wmod=np.ascontiguousarray(np.asarray(inp["w_mod"], f32)),
                win=win_r, wout=wout_r, wmi=wmi_r, wmo=wmo_r, bd=bd, wst=wst, bsb=bsb)


def kernel(**inp):
    nstage = int(os.environ.get("MK_NSTAGE", "6"))
    debug_ctx = os.environ.get("MK_DEBUG_CTX", "0") == "1"
    import time as _time0
    _tp = _time0.time()
    shared = prep_shared(inp)
    if os.environ.get("MK_VERBOSE"):
        print("prep %.1fs" % (_time0.time() - _tp), flush=True)
    x = np.asarray(inp["x"], np.float32)
    ctx = np.asarray(inp["ctx"], np.float32)
    c = np.asarray(inp["c"], np.float32)
    c_ctx = np.asarray(inp["c_ctx"], np.float32)
    in_maps = []
    for b in range(8):
        cv = np.zeros((P, KD, 2), np.float32)
        cv[:, :, 0] = c[b].reshape(KD, P).T
        cv[:, :, 1] = c_ctx.reshape(KD, P).T
        m = dict(shared)
        m["x"] = np.ascontiguousarray(x[b])
        m["ctx"] = np.ascontiguousarray(ctx[b])
        m["cv"] = cv.reshape(P, 16)
        in_maps.append(m)
    import time as _time
    _t0 = _time.time()
    nc = build(nstage, debug_ctx)
    _t1 = _time.time()
    res = run_bass_kernel_spmd(nc, in_maps, core_ids=list(range(8)))
    if os.environ.get("MK_VERBOSE"):
        print("build %.1fs run %.1fs" % (_t1 - _t0, _time.time() - _t1), flush=True)
    out = np.stack([np.asarray(r["out"], np.float32) for r in res.results], axis=0)
    if debug_ctx:
        kernel.last_ctx = np.stack([np.asarray(r["outc"], np.float32) for r in res.results], axis=0)
    return out
```
